# Optimizing a Trainium2 kernel written in Bass

```python
import jax, jax.numpy as jnp
from jax import lax
import numpy as np

D_MODEL = 1024
BATCH = 2
SEQ = 8192
DEPTH = 1
DEC_BATCH = 32
DEC_SEQ = 8
PAST_LEN = 16384
PAGE_SIZE = 128

HEAD_DIM = 64
A_WIDTH = 3 * D_MODEL // 8
A_HEADS = A_WIDTH // HEAD_DIM
M_WIDTH = D_MODEL // 4
M_HEADS = 4
M_HEAD_DIM = M_WIDTH // M_HEADS
C_WIDTH = D_MODEL - A_WIDTH - M_WIDTH
D_MIX = A_WIDTH + C_WIDTH + M_WIDTH
ROT_DIM = HEAD_DIM // 4
ROPE_THETA = 500000.0
DILATED_PAIRS = ((128, 1), (512, 4), (2048, 16))
MAX_WINDOW = 2048
CONV_W = 31
N_MEM = 256
Q_BLOCK = 128
EPS = 1e-6
SPLIT_WIDTHS = (A_WIDTH, A_WIDTH, A_WIDTH, A_WIDTH, C_WIDTH, C_WIDTH, C_WIDTH, M_WIDTH, M_WIDTH)
D_IN = sum(SPLIT_WIDTHS)
SPLIT_POINTS = tuple(sum(SPLIT_WIDTHS[:i + 1]) for i in range(len(SPLIT_WIDTHS) - 1))

kernel_name = 'dilated_conformer_memory_hybrid'


def rmsnorm(x, g):
    xf = x.astype(jnp.float32)
    y = xf * lax.rsqrt(jnp.mean(xf * xf, axis=-1, keepdims=True) + EPS) * g.astype(jnp.float32)
    return y.astype(x.dtype)


def rope(x, pos):
    inv = ROPE_THETA ** (-jnp.arange(0, ROT_DIM, 2, dtype=jnp.float32) / ROT_DIM)
    ang = pos.astype(jnp.float32)[:, None] * inv[None, :]
    cos = jnp.cos(ang)[None, :, None, :]
    sin = jnp.sin(ang)[None, :, None, :]
    half = ROT_DIM // 2
    x1 = x[..., :half].astype(jnp.float32)
    x2 = x[..., half:ROT_DIM].astype(jnp.float32)
    rot = jnp.concatenate([x1 * cos - x2 * sin, x2 * cos + x1 * sin], axis=-1).astype(x.dtype)
    return jnp.concatenate([rot, x[..., ROT_DIM:]], axis=-1)


def dilated_mix(q, qrow, k, v):
    scale = HEAD_DIM ** -0.5
    outs, lses = [], []
    for window, dil in DILATED_PAIRS:
        steps = jnp.arange(window // dil + 1, dtype=jnp.int32)
        idx = qrow[:, None] - dil * steps[None, :]
        valid = idx >= 0
        idx = jnp.maximum(idx, 0)
        kg = jnp.take(k, idx, axis=1)
        vg = jnp.take(v, idx, axis=1)
        s = jnp.einsum('bqhd,bqmhd->bhqm', q, kg, preferred_element_type=jnp.float32) * scale
        s = jnp.where(valid[None, None], s, -jnp.inf)
        lse = jax.nn.logsumexp(s, axis=-1)
        p = jnp.exp(s - lse[..., None])
        o = jnp.einsum('bhqm,bqmhd->bqhd', p.astype(v.dtype), vg, preferred_element_type=jnp.float32)
        outs.append(o)
        lses.append(lse)
    wts = jax.nn.softmax(jnp.stack(lses, axis=0), axis=0)
    wts = jnp.transpose(wts, (0, 1, 3, 2))[..., None]
    out = wts[0] * outs[0]
    for i in range(1, len(outs)):
        out = out + wts[i] * outs[i]
    return out.astype(q.dtype)


def prompt_dilated(q, k, v):
    b, s, h, dh = q.shape
    nb = s // Q_BLOCK
    qb = jnp.transpose(q.reshape(b, nb, Q_BLOCK, h, dh), (1, 0, 2, 3, 4))

    def one_block(args):
        q_blk, n = args
        qrow = n * Q_BLOCK + jnp.arange(Q_BLOCK, dtype=jnp.int32)
        return dilated_mix(q_blk, qrow, k, v)

    o = lax.map(one_block, (qb, jnp.arange(nb, dtype=jnp.int32)))
    return jnp.transpose(o, (1, 0, 2, 3, 4)).reshape(b, s, h, dh)


def depthwise_causal(u_ext, w_dw):
    c = u_ext.shape[-1]
    return lax.conv_general_dilated(u_ext, w_dw[:, None, :].astype(u_ext.dtype), (1,), 'VALID',
                                    dimension_numbers=('NWC', 'WIO', 'NWC'), feature_group_count=c)


def conformer_tail(c, b_dw, ln_g, ln_b, w_pw2, b_pw2):
    cf = c.astype(jnp.float32) + b_dw.astype(jnp.float32)
    mu = jnp.mean(cf, axis=-1, keepdims=True)
    var = jnp.mean(jnp.square(cf - mu), axis=-1, keepdims=True)
    cn = (cf - mu) * lax.rsqrt(var + EPS) * ln_g.astype(jnp.float32) + ln_b.astype(jnp.float32)
    sw = jax.nn.silu(cn).astype(c.dtype)
    return sw @ w_pw2 + b_pw2


def mem_kv(mem, norm_mem, w_mem_kv):
    b, n, _ = mem.shape
    kv = (rmsnorm(mem, norm_mem) @ w_mem_kv).reshape(b, n, 2, M_HEADS, M_HEAD_DIM)
    return kv[:, :, 0], kv[:, :, 1]


def mem_attend(qm, mk, mv):
    s = jnp.einsum('bthd,bnhd->bhtn', qm, mk, preferred_element_type=jnp.float32) * (M_HEAD_DIM ** -0.5)
    p = jax.nn.softmax(s, axis=-1)
    return jnp.einsum('bhtn,bnhd->bthd', p.astype(mv.dtype), mv)


def project(h, w_in, pos):
    b, t, _ = h.shape
    qa, ka, va, ga, ab, bb, gb, qm, gm = jnp.split(h @ w_in, SPLIT_POINTS, axis=-1)
    qa = rope(qa.reshape(b, t, A_HEADS, HEAD_DIM), pos)
    ka = rope(ka.reshape(b, t, A_HEADS, HEAD_DIM), pos)
    va = va.reshape(b, t, A_HEADS, HEAD_DIM)
    u = ab * jax.nn.sigmoid(bb)
    qm = qm.reshape(b, t, M_HEADS, M_HEAD_DIM)
    return qa, ka, va, ga, u, gb, qm, gm


def combine(x, oa, ga, ob, gb, om, gm, w_out, norm_post):
    b, t, _ = x.shape
    mixed = jnp.concatenate([oa.reshape(b, t, A_WIDTH) * jax.nn.silu(ga),
                             ob * jax.nn.silu(gb),
                             om.reshape(b, t, M_WIDTH) * jax.nn.silu(gm)], axis=-1)
    return x + rmsnorm(mixed @ w_out, norm_post)


def setup_inputs(seed: int = 0) -> dict:
    key = jax.random.key(seed)
    ks = jax.random.split(key, 20)
    win_len = min(MAX_WINDOW, PAST_LEN)
    nrm = jax.random.normal
    f32 = jnp.float32
    return {
        'x_prompt': nrm(ks[0], (BATCH, SEQ, D_MODEL), f32),
        'x_sample': nrm(ks[1], (DEC_BATCH, DEC_SEQ, D_MODEL), f32),
        'cache_win_k': nrm(ks[2], (DEPTH, DEC_BATCH, win_len, A_HEADS, HEAD_DIM), f32),
        'cache_win_v': nrm(ks[3], (DEPTH, DEC_BATCH, win_len, A_HEADS, HEAD_DIM), f32),
        'state_conv': 0.5 * nrm(ks[4], (DEPTH, DEC_BATCH, CONV_W - 1, C_WIDTH), f32),
        'cache_mem_k': nrm(ks[5], (DEPTH, DEC_BATCH, N_MEM, M_HEADS, M_HEAD_DIM), f32),
        'cache_mem_v': nrm(ks[6], (DEPTH, DEC_BATCH, N_MEM, M_HEADS, M_HEAD_DIM), f32),
        'mem_prompt': nrm(ks[7], (BATCH, N_MEM, D_MODEL), f32),
        'norm_pre': 1.0 + 0.05 * nrm(ks[8], (DEPTH, D_MODEL), f32),
        'norm_post': 1.0 + 0.05 * nrm(ks[9], (DEPTH, D_MODEL), f32),
        'w_in': nrm(ks[10], (DEPTH, D_MODEL, D_IN), f32) * D_MODEL ** -0.5,
        'w_out': nrm(ks[11], (DEPTH, D_MIX, D_MODEL), f32) * D_MIX ** -0.5,
        'norm_mem': 1.0 + 0.05 * nrm(ks[12], (DEPTH, D_MODEL), f32),
        'w_mem_kv': nrm(ks[13], (DEPTH, D_MODEL, 2 * M_WIDTH), f32) * D_MODEL ** -0.5,
        'w_dw': nrm(ks[14], (DEPTH, CONV_W, C_WIDTH), f32) * CONV_W ** -0.5,
        'b_dw': 0.02 * nrm(ks[15], (DEPTH, C_WIDTH), f32),
        'ln_conv_g': 1.0 + 0.05 * nrm(ks[16], (DEPTH, C_WIDTH), f32),
        'ln_conv_b': 0.02 * nrm(ks[17], (DEPTH, C_WIDTH), f32),
        'w_pw2': nrm(ks[18], (DEPTH, C_WIDTH, C_WIDTH), f32) * C_WIDTH ** -0.5,
        'b_pw2': 0.02 * nrm(ks[19], (DEPTH, C_WIDTH), f32),
    }


def reference(x_prompt, x_sample, cache_win_k, cache_win_v, state_conv, cache_mem_k, cache_mem_v, mem_prompt,
              norm_pre, norm_post, w_in, w_out, norm_mem, w_mem_kv, w_dw, b_dw, ln_conv_g, ln_conv_b, w_pw2, b_pw2):
    s_p = x_prompt.shape[1]
    t_s = x_sample.shape[1]
    pos_p = jnp.arange(s_p, dtype=jnp.int32)
    pos_s = PAST_LEN + jnp.arange(t_s, dtype=jnp.int32)
    keep_p = min(MAX_WINDOW, s_p)
    keep_s = min(MAX_WINDOW, PAST_LEN + t_s)
    xp, xs = x_prompt, x_sample
    wk_p, wv_p, cv_p, mk_p, mv_p, wk_s, wv_s, cv_s = [], [], [], [], [], [], [], []
    for l in range(DEPTH):
        hp = rmsnorm(xp, norm_pre[l])
        qa, ka, va, ga, u, gb, qm, gm = project(hp, w_in[l], pos_p)
        oa = prompt_dilated(qa, ka, va)
        u_ext = jnp.pad(u, ((0, 0), (CONV_W - 1, 0), (0, 0)))
        ob = conformer_tail(depthwise_causal(u_ext, w_dw[l]), b_dw[l], ln_conv_g[l], ln_conv_b[l], w_pw2[l], b_pw2[l])
        mk, mv = mem_kv(mem_prompt, norm_mem[l], w_mem_kv[l])
        om = mem_attend(qm, mk, mv)
        xp = combine(xp, oa, ga, ob, gb, om, gm, w_out[l], norm_post[l])
        wk_p.append(ka[:, s_p - keep_p:])
        wv_p.append(va[:, s_p - keep_p:])
        cv_p.append(u[:, s_p - (CONV_W - 1):])
        mk_p.append(mk)
        mv_p.append(mv)
        hs = rmsnorm(xs, norm_pre[l])
        qa, ka, va, ga, u, gb, qm, gm = project(hs, w_in[l], pos_s)
        kb = jnp.concatenate([cache_win_k[l], ka], axis=1)
        vb = jnp.concatenate([cache_win_v[l], va], axis=1)
        win_len = cache_win_k.shape[2]
        qrow = win_len + jnp.arange(t_s, dtype=jnp.int32)
        oa = dilated_mix(qa, qrow, kb, vb)
        uc = jnp.concatenate([state_conv[l], u], axis=1)
        ob = conformer_tail(depthwise_causal(uc, w_dw[l]), b_dw[l], ln_conv_g[l], ln_conv_b[l], w_pw2[l], b_pw2[l])
        om = mem_attend(qm, cache_mem_k[l], cache_mem_v[l])
        xs = combine(xs, oa, ga, ob, gb, om, gm, w_out[l], norm_post[l])
        tot = kb.shape[1]
        wk_s.append(kb[:, tot - keep_s:])
        wv_s.append(vb[:, tot - keep_s:])
        cv_s.append(uc[:, uc.shape[1] - (CONV_W - 1):])
    return (xp, xs, jnp.stack(wk_p), jnp.stack(wv_p), jnp.stack(cv_p), jnp.stack(mk_p), jnp.stack(mv_p),
            jnp.stack(wk_s), jnp.stack(wv_s), jnp.stack(cv_s))
```

```python
import math
from contextlib import ExitStack
import numpy as np
import concourse.bass as bass
import concourse.mybir as mybir
from concourse.bass_utils import run_bass_kernel_spmd

F32 = mybir.dt.float32
BF16 = mybir.dt.bfloat16
I32 = mybir.dt.int32
AF = mybir.ActivationFunctionType
ALU = mybir.AluOpType
AX = mybir.AxisListType
EPS = 1e-6
TWO_PI = 2.0 * math.pi
C1 = 6.28125
C2 = TWO_PI - C1
PI_CL = 3.1415925


class Sched:
    def __init__(self, nc, stack, n_dma_sems=10):
        self.nc = nc
        self.h = {"pe": nc.tensor, "act": nc.scalar, "dve": nc.vector, "pool": nc.gpsimd, "sp": nc.sync}
        self.sem = {k: stack.enter_context(nc.semaphore("s_" + k)) for k in self.h}
        self.cnt = {k: 0 for k in self.h}
        self.waited = {k: {} for k in self.h}
        self.lastw = {}
        self.readers = {}
        self.dma_sems = {}
        for q in ("sp", "act", "pool"):
            self.dma_sems[q] = [[stack.enter_context(nc.semaphore("d_%s_%d" % (q, i))), 0] for i in range(n_dma_sems)]
        self.dma_sems["bulk"] = [[stack.enter_context(nc.semaphore("d_bulk_%d" % i)), 0] for i in range(24)]
        self.dma_rr = {q: 0 for q in self.dma_sems}
        self.n_inst = 0
        self.dead = False

    def _wait(self, eng, ev):
        sem, val, src = ev
        if src == eng and eng == "pe":
            return
        key = id(sem)
        if self.waited[eng].get(key, 0) >= val:
            return
        self.h[eng].wait_ge(sem, val)
        self.waited[eng][key] = val

    def _deps(self, eng, reads, writes):
        for b in reads:
            ev = self.lastw.get(b)
            if ev is not None:
                self._wait(eng, ev)
        for b in writes:
            ev = self.lastw.get(b)
            if ev is not None:
                self._wait(eng, ev)
            for ev in self.readers.get(b, ()):
                self._wait(eng, ev)

    def _record(self, ev, reads, writes):
        for b in reads:
            self.readers.setdefault(b, []).append(ev)
        for b in writes:
            self.lastw[b] = ev
            self.readers[b] = []

    def op(self, eng, fn, reads=(), writes=()):
        if self.dead:
            return None
        pr = [b for b in reads if b[:2] == "pB"]
        if pr:
            reads = [b for b in reads if b not in pr]
            writes = list(writes) + pr
        self._deps(eng, reads, writes)
        inst = fn(self.h[eng])
        self.cnt[eng] += 1
        inst.then_inc(self.sem[eng], 1)
        ev = (self.sem[eng], self.cnt[eng], eng)
        self._record(ev, reads, writes)
        self.n_inst += 1
        return ev

    def dma(self, q, out, in_, reads=(), writes=(), sems=None, **kw):
        if self.dead:
            return None
        self._deps(q, reads, writes)
        sq = sems or q
        slot = self.dma_sems[sq][self.dma_rr[sq] % len(self.dma_sems[sq])]
        self.dma_rr[sq] += 1
        sem, val = slot
        if val > 0:
            self._wait(q, (sem, val, "dma"))
        inst = self.h[q].dma_start(out=out, in_=in_, **kw)
        slot[1] = val + 16
        inst.then_inc(sem, 16)
        ev = (sem, val + 16, "dma")
        self._record(ev, reads, writes)
        self.n_inst += 1
        return ev

    def barrier(self):
        if self.dead:
            return
        for e in self.h:
            for k in self.h:
                if k != e and self.cnt[k] > 0:
                    self._wait(e, (self.sem[k], self.cnt[k], k))
            for q in self.dma_sems:
                for sem, val in self.dma_sems[q]:
                    if val > 0:
                        self._wait(e, (sem, val, "dma"))
        self.lastw = {}
        self.readers = {}

    def finish(self, eng="sp"):
        for q in self.dma_sems:
            for sem, val in self.dma_sems[q]:
                if val > 0:
                    self.waited[eng].pop(id(sem), None)
                    self.h[eng].wait_ge(sem, val)
        for k in self.h:
            if k != eng and self.cnt[k] > 0:
                self.h[eng].wait_ge(self.sem[k], self.cnt[k])


class _Stop(Exception):
    pass


def sl(start, n, step):
    return slice(start, start + step * (n - 1) + 1, step)


def build_nc():
    nc = bass.Bass("TRN2", target_bir_lowering=False)

    def DI(name, shape):
        return nc.dram_tensor(name, shape, F32, kind="ExternalInput").ap()

    def DO(name, shape):
        return nc.dram_tensor(name, shape, F32, kind="ExternalOutput").ap()

    xe = DI("xe", [4096, 1024]); pos = DI("pos", [1, 4096]); hflag = DI("hflag", [128, 1])
    rcst = DI("rcst", [128, 2]); xs = DI("xs", [32, 1024]); rcs = DI("rcs", [32, 49])
    ck = DI("ck", [4, 2048, 384]); cv = DI("cv", [4, 2048, 384]); sc = DI("sc", [4, 30, 384])
    cmk = DI("cmk", [4, 256, 256]); cmv = DI("cmv", [4, 256, 256]); mem = DI("mem", [256, 1024])
    w_in = DI("w_in", [1024, 3200]); w_out = DI("w_out", [1024, 1024]); w_mkv = DI("w_mkv", [1024, 512])
    w_dwT = DI("w_dwT", [128, 3, 31]); w_pw2 = DI("w_pw2", [384, 384])
    g_pre = DI("g_pre", [1, 1024]); g_post = DI("g_post", [1, 1024]); g_mem = DI("g_mem", [1, 1024])
    vecs = DI("vecs", [128, 4, 3])
    yp = DO("yp", [2048, 1024]); ys = DO("ys", [32, 1024])
    kp = DO("kp", [2048, 384]); vp = DO("vp", [2048, 384]); cvp = DO("cvp", [30, 384])
    mkp = DO("mkp", [256, 256]); mvp = DO("mvp", [256, 256])
    ksn = DO("ksn", [4, 2048, 384]); vsn = DO("vsn", [4, 2048, 384]); csn = DO("csn", [4, 30, 384])

    w_in_v = w_in.rearrange("(c p) n -> p c n", p=128)

    import os
    STOP = float(os.environ.get("KSTOP", "99"))
    with ExitStack() as st:
        S = Sched(nc, st)

        def chk(level):
            if STOP <= level:
                S.barrier()
                S.dead = True

        def sbt(stack, name, shape, dt):
            return stack.enter_context(nc.sbuf_tensor(name, shape, dt))

        def pst(name, shape, dt):
            return st.enter_context(nc.psum_tensor(name, shape, dt))

        PB = [pst("pB%d" % i, [128, 512], F32) for i in range(8)]
        roles = {"pT": [0, 1], "pA": [2, 3], "pS": [4, 5], "pO": [6, 7]}
        rr = {"pT": 0, "pA": 0, "pS": 0, "pO": 0, "cp": 0, "w": 0, "kb": 0, "eb": 0, "rzs": 0}

        def nxt(k, n=2):
            if k in roles:
                v = roles[k][rr[k] % len(roles[k])]
            else:
                v = rr[k] % n
            rr[k] += 1
            return v

        def PA(b):
            return PB[b]

        def PT(b):
            return PB[b][:].bitcast(BF16)

        def cp(out, in_, reads, writes):
            if nxt("cp") == 0:
                return S.op("act", lambda e: e.copy(out, in_), reads=reads, writes=writes)
            return S.op("dve", lambda e: e.tensor_copy(out=out, in_=in_), reads=reads, writes=writes)

        ident = sbt(st, "ident", [128, 128], BF16)
        onesb = sbt(st, "onesb", [128, 128], BF16)
        mask4 = sbt(st, "mask4", [128, 512], BF16)
        Rm = sbt(st, "Rm", [128, 128], BF16)
        mhalf = sbt(st, "mhalf", [128, 8], F32)
        epst = sbt(st, "epst", [128, 1], F32)
        rc_sb = sbt(st, "rc_sb", [128, 2], F32)
        vec_sb = sbt(st, "vec_sb", [128, 4, 3], F32)
        hfl = sbt(st, "hfl", [128, 1], F32)
        ss = sbt(st, "ss", [128, 64], F32)
        ms = sbt(st, "ms", [128, 64], F32)
        rstd = sbt(st, "rstd", [128, 64], F32)
        hTh32 = sbt(st, "hTh32", [128, 8, 32], BF16)
        hTo = sbt(st, "hTo", [128, 8, 2048], BF16)
        hTs = sbt(st, "hTs", [128, 8, 32], BF16)
        hTm = sbt(st, "hTm", [128, 8, 256], BF16)
        mixT = sbt(st, "mixT", [128, 8, 2048], BF16)
        mixTs = sbt(st, "mixTs", [128, 8, 32], BF16)
        wst = [sbt(st, "wst%d" % i, [128, 8, 128], BF16) for i in range(4)]
        ts32 = sbt(st, "ts32", [32, 3200], F32)

        S.op("pool", lambda e: e.memset(onesb[:], 1.0), writes=["onesb"])
        S.op("pool", lambda e: e.memset(mhalf[:], -0.5), writes=["mhalf"])
        S.op("pool", lambda e: e.memset(epst[:], EPS), writes=["epst"])
        S.op("pool", lambda e: e.memset(ss[:], 0.0), writes=["ss"])
        S.op("pool", lambda e: e.affine_select(out=ident[:], in_=onesb[:], pattern=[[1, 128]], compare_op=ALU.is_equal,
                                               fill=0.0, base=0, channel_multiplier=-1), reads=["onesb"], writes=["ident"])
        for blk in range(4):
            if blk % 2 == 0:
                S.op("pool", lambda e, blk=blk: e.affine_select(out=mask4[:, 128 * blk:128 * blk + 128], in_=onesb[:], pattern=[[-1, 128]],
                                                                compare_op=ALU.is_ge, fill=0.0, base=0, channel_multiplier=1),
                     reads=["onesb"], writes=["mask4"])
            else:
                S.op("pool", lambda e, blk=blk: e.affine_select(out=mask4[:, 128 * blk:128 * blk + 128], in_=onesb[:], pattern=[[1, 128]],
                                                                compare_op=ALU.is_ge, fill=0.0, base=0, channel_multiplier=-1),
                     reads=["onesb"], writes=["mask4"])
        S.op("pool", lambda e: e.memset(Rm[:], 0.0), writes=["Rm"])
        for hb in range(2):
            c0 = 64 * hb
            S.op("pool", lambda e, c0=c0: e.affine_select(out=Rm[:, c0:c0 + 8], in_=onesb[:, 0:8], pattern=[[-1, 8]], compare_op=ALU.is_equal,
                                                          fill=0.0, base=-c0 - 8, channel_multiplier=1), reads=["onesb"], writes=["Rm"])
            S.op("pool", lambda e, c0=c0: e.affine_select(out=Rm[:, c0 + 8:c0 + 16], in_=onesb[:, 0:8], pattern=[[-1, 8]], compare_op=ALU.is_equal,
                                                          fill=0.0, base=-c0, channel_multiplier=1), reads=["onesb"], writes=["Rm"])
        S.dma("sp", rc_sb[:], rcst[:], writes=["rc_sb"])
        S.dma("sp", vec_sb[:], vecs[:], writes=["vec_sb"])
        S.dma("sp", hfl[:], hflag[:], writes=["hfl"])

        bulk_jobs = []
        for bl_ in range(4):
            for q4 in range(4):
                r0 = 510 * q4
                bulk_jobs.append((ksn[bl_, r0:r0 + 510, :], ck[bl_, 8 + r0:8 + r0 + 510, :]))
                bulk_jobs.append((vsn[bl_, r0:r0 + 510, :], cv[bl_, 8 + r0:8 + r0 + 510, :]))
            bulk_jobs.append((csn[bl_, 0:22, :], sc[bl_, 8:30, :]))

        def bulk_issue(n):
            for _ in range(n):
                if bulk_jobs:
                    o_, i_ = bulk_jobs.pop(0)
                    S.dma("sp", o_, i_, sems="bulk")

        NW = len(wst)
        wq_cols = []
        for hp_ in range(3):
            wq_cols += [384 + 128 * hp_, 128 * hp_, 768 + 128 * hp_, 1152 + 128 * hp_]
        for cc_ in range(3):
            wq_cols += [1536 + 128 * cc_, 1920 + 128 * cc_, 2304 + 128 * cc_]
        for mp_ in range(2):
            wq_cols += [2688 + 128 * mp_, 2944 + 128 * mp_]
        wstate = {"issued": 0, "used": 0}

        def w_issue(upto):
            while wstate["issued"] < min(upto, len(wq_cols)):
                i = wstate["issued"]
                s_ = i % NW
                S.dma("pool", wst[s_][:], w_in_v[:, :, wq_cols[i]:wq_cols[i] + 128], writes=["wst%d" % s_])
                wstate["issued"] += 1

        def wtile(col0):
            i = wstate["used"]
            assert wq_cols[i] == col0, (i, col0)
            wstate["used"] += 1
            if i >= 1:
                bulk_issue(2)
            w_issue(i + 1)
            w_issue(i + NW - 1)
            return i % NW

        def hT_rhs(c, g):
            if g < 4:
                return hTh[:, c, 512 * g:512 * g + 512]
            return hTo[:, c, 512 * (g - 4):512 * (g - 4) + 512]

        def proj(ws, g, pa, ntok=512, rhs_fn=None):
            for c in range(8):
                rhs = rhs_fn(c) if rhs_fn is not None else hT_rhs(c, g)
                S.op("pe", lambda e, c=c, rhs=rhs: e.matmul(PA(pa)[:, 0:ntok], wst[ws][:, c, :], rhs, start=(c == 0), stop=(c == 7)),
                     reads=["wst%d" % ws, "hT"], writes=["pB%d" % pa])

        def proj_s(ws, col0):
            pa = nxt("pA")
            for c in range(8):
                S.op("pe", lambda e, c=c: e.matmul(PA(pa)[0:32, 0:128], hTs[:, c, :], wst[ws][:, c, :], start=(c == 0), stop=(c == 7)),
                     reads=["wst%d" % ws, "hT"], writes=["pB%d" % pa])
            cp(ts32[:, col0:col0 + 128], PA(pa)[0:32, 0:128], ["pB%d" % pa], ["ts32"])

        try:
            w_issue(NW - 1)
            with ExitStack() as ph:
                hTh = sbt(ph, "hTh", [128, 8, 2048], BF16)
                cosT = sbt(ph, "cosT", [128, 4096], BF16)
                sinT = sbt(ph, "sinT", [128, 4096], BF16)
                with ExitStack() as p1:
                    xt = [sbt(p1, "xt%d" % i, [128, 1024], F32) for i in range(6)]
                    xn = [sbt(p1, "xn%d" % i, [128, 1024], BF16) for i in range(3)]
                    roles.update({"pT": [0, 1, 2, 3]})
                    junk3 = [sbt(p1, "jnk%d" % i, [128, 1024], BF16) for i in range(3)]
                    gpre = sbt(p1, "gpre", [128, 1024], F32)
                    gmem = sbt(p1, "gmem", [128, 1024], F32)
                    S.dma("sp", gpre[:], g_pre[0:1, :].to_broadcast([128, 1024]), writes=["gpre"])
                    S.dma("sp", gmem[:], g_mem[0:1, :].to_broadcast([128, 1024]), writes=["gmem"])
                    posb = sbt(p1, "posb", [128, 512], F32)
                    ta = sbt(p1, "ta", [128, 512], F32)
                    tb = sbt(p1, "tb", [128, 512], F32)
                    tk = sbt(p1, "tk", [128, 512], I32)
                    tf = sbt(p1, "tf", [128, 512], F32)
                    tk2 = sbt(p1, "tk2", [128, 512], I32)
                    tf2 = sbt(p1, "tf2", [128, 512], F32)

                    def table_chunk(ch):
                        cs = slice(512 * ch, 512 * ch + 512)
                        S.dma("sp", posb[:], pos[0:1, cs].to_broadcast([128, 512]), writes=["posb"])
                        S.op("dve", lambda e: e.tensor_scalar(out=ta[:], in0=posb[:], scalar1=rc_sb[:, 0:1], scalar2=None, op0=ALU.mult),
                             reads=["posb", "rc_sb"], writes=["ta"])
                        S.op("dve", lambda e: e.tensor_scalar(out=tb[:], in0=ta[:], scalar1=math.pi / 2, scalar2=None, op0=ALU.add), reads=["ta"], writes=["tb"])
                        ch_ = [(ta, "ta", tk, "tk", tf, "tf"), (tb, "tb", tk2, "tk2", tf2, "tf2")]
                        for (src, sname, k_, kn, f_, fn) in ch_:
                            S.op("dve", lambda e, src=src, k_=k_: e.tensor_scalar(out=k_[:], in0=src[:], scalar1=1.0 / TWO_PI, scalar2=None, op0=ALU.mult), reads=[sname], writes=[kn])
                        for (src, sname, k_, kn, f_, fn) in ch_:
                            S.op("dve", lambda e, k_=k_, f_=f_: e.tensor_copy(out=f_[:], in_=k_[:]), reads=[kn], writes=[fn])
                        for cst in (-C1, -C2):
                            for (src, sname, k_, kn, f_, fn) in ch_:
                                S.op("dve", lambda e, src=src, f_=f_, cst=cst: e.scalar_tensor_tensor(out=src[:], in0=f_[:], scalar=cst, in1=src[:], op0=ALU.mult, op1=ALU.add),
                                     reads=[fn, sname], writes=[sname])
                        for (src, sname, k_, kn, f_, fn) in ch_:
                            S.op("dve", lambda e, src=src: e.tensor_scalar(out=src[:], in0=src[:], scalar1=-PI_CL, scalar2=PI_CL, op0=ALU.max, op1=ALU.min), reads=[sname], writes=[sname])
                        S.op("act", lambda e: e.activation(out=sinT[:, cs], in_=ta[:], func=AF.Sin, scale=rc_sb[:, 1:2]), reads=["ta", "rc_sb"], writes=["sinT"])
                        S.op("act", lambda e: e.activation(out=cosT[:, cs], in_=tb[:], func=AF.Sin), reads=["tb"], writes=["cosT"])
                    cnt = [0]

                    def norm_a(job):
                        src, npart, gt, gname, dst = job
                        i = cnt[0]
                        cnt[0] += 1
                        xs_, ns_ = i % 6, i % 3
                        S.dma("sp", xt[xs_][0:npart, :], src, writes=["xt%d" % xs_])
                        S.op("act", lambda e: e.activation(out=junk3[ns_][0:npart, :], in_=xt[xs_][0:npart, :], func=AF.Square, accum_out=ss[0:npart, i:i + 1]),
                             reads=["xt%d" % xs_, "ss"], writes=["junk%d" % ns_, "ss%d" % i])
                        S.op("pool", lambda e: e.tensor_scalar(out=ms[0:npart, i:i + 1], in0=ss[0:npart, i:i + 1], scalar1=1.0 / 1024, scalar2=EPS,
                                                               op0=ALU.mult, op1=ALU.add), reads=["ss%d" % i], writes=["ms%d" % i])
                        S.op("pool", lambda e: e.tensor_tensor(out=rstd[0:npart, i:i + 1], in0=ms[0:npart, i:i + 1], in1=mhalf[0:npart, 0:1], op=ALU.pow),
                             reads=["ms%d" % i, "mhalf"], writes=["rstd%d" % i])
                        S.op("dve", lambda e: e.scalar_tensor_tensor(out=xn[ns_][0:npart, :], in0=xt[xs_][0:npart, :], scalar=rstd[0:npart, i:i + 1],
                                                                     in1=gt[0:npart, :], op0=ALU.mult, op1=ALU.mult),
                             reads=["xt%d" % xs_, "rstd%d" % i, gname], writes=["xn%d" % ns_])
                        return ns_

                    def norm_b(job, ns_):
                        src, npart, gt, gname, dst = job
                        ps_ = nxt("pT")
                        for c in range(8):
                            S.op("pe", lambda e, c=c: e.transpose(PT(ps_)[:, c * 128:c * 128 + npart], xn[ns_][0:npart, c * 128:(c + 1) * 128],
                                                                  ident[0:npart, 0:npart]),
                                 reads=["xn%d" % ns_, "ident"], writes=["pB%d" % ps_])
                        src_ = PT(ps_)[:, :].rearrange("p (c t) -> p c t", t=128)[:, :, 0:npart]
                        S.op("act", lambda e: e.copy(dst, src_), reads=["pB%d" % ps_], writes=["hT"])

                    jobs = []
                    for tt in range(32):
                        if tt < 16:
                            dst = hTh[:, :, 128 * tt:128 * tt + 128]
                        else:
                            dst = hTo[:, :, 128 * (tt - 16):128 * (tt - 16) + 128]
                        jobs.append((xe[128 * tt:128 * tt + 128, :], 128, gpre, "gpre", dst))
                    jobs.append((xs[:, :], 32, gpre, "gpre", hTs[:, :, :]))
                    for mt in range(2):
                        jobs.append((mem[128 * mt:128 * mt + 128, :], 128, gmem, "gmem", hTm[:, :, 128 * mt:128 * mt + 128]))
                    NLAG = 2
                    nsl = {}
                    for i in range(len(jobs) + NLAG):
                        if i < len(jobs):
                            nsl[i] = norm_a(jobs[i])
                        if i >= NLAG:
                            norm_b(jobs[i - NLAG], nsl[i - NLAG])
                        if i < 32 and i % 4 == 3:
                            table_chunk(i // 4)
                    S.op("pool", lambda e: e.tensor_copy(out=hTh32[:], in_=hTh[:, :, 2016:2048]), reads=["hT"], writes=["hTh32"])
                    S.barrier()
                    chk(1)

                with ExitStack() as p2:
                    chk(2)

                    kT = sbt(p2, "kT", [128, 4096], BF16)
                    qT = sbt(p2, "qT", [128, 2048], BF16)
                    vT = sbt(p2, "vT", [128, 4096], BF16)
                    kb = [sbt(p2, "kb%d" % i, [128, 512], BF16) for i in range(2)]
                    t1 = sbt(p2, "t1", [128, 512], F32)
                    t2 = sbt(p2, "t2", [128, 512], F32)
                    vx = sbt(p2, "vx", [128, 69, 3, 64], BF16)
                    vxf = vx[:].rearrange("p t b d -> p t (b d)")
                    acc = sbt(p2, "acc", [128, 2048], F32)
                    rz = sbt(p2, "rz", [128, 256], F32)
                    tmpn = sbt(p2, "tmpn", [128, 256], F32)
                    Eb = [sbt(p2, "Eb%d" % i, [128, 512], BF16) for i in range(4)]
                    ko32 = sbt(p2, "ko32", [128, 2, 128], F32)

                    S.op("pool", lambda e: e.memset(vx[:, :, 1, :], 1.0), writes=["vx1"])
                    halo_tiles = [0] + [17 + 5 * r for r in range(4)] + [37 + 2 * r for r in range(16)]
                    for ti in halo_tiles:
                        S.op("pool", lambda e, ti=ti: e.tensor_scalar(out=vx[:, ti, 1, :], in0=vx[:, ti, 1, :], scalar1=hfl[:, 0:1], scalar2=None, op0=ALU.mult),
                             reads=["vx1", "hfl"], writes=["vx1"])

                    def rope_seq(ws, groups, dst, dcol0):
                        st_ = {}

                        def stage_a(g):
                            pa = nxt("pA")
                            proj(ws, g, pa)
                            kbs = nxt("kb")
                            S.op("act", lambda e: e.copy(kb[kbs][:], PA(pa)[:]), reads=["pB%d" % pa], writes=["kb%d" % kbs])
                            st_[g] = (pa, kbs)

                        def stage_b(g):
                            pa, kbs = st_[g]
                            pb = nxt("pA")
                            S.op("pe", lambda e: e.matmul(PA(pb)[:], Rm[:], kb[kbs][:], start=True, stop=True), reads=["Rm", "kb%d" % kbs], writes=["pB%d" % pb])
                            tok = slice(512 * g, 512 * g + 512)
                            dcol = 512 * g - dcol0
                            S.op("dve", lambda e: e.tensor_tensor(out=t1[:], in0=PA(pa)[:], in1=cosT[:, tok], op=ALU.mult), reads=["pB%d" % pa, "cosT"], writes=["t1"])
                            S.op("dve", lambda e: e.tensor_tensor(out=t2[:], in0=PA(pb)[:], in1=sinT[:, tok], op=ALU.mult), reads=["pB%d" % pb, "sinT"], writes=["t2"])
                            S.op("dve", lambda e: e.tensor_tensor(out=dst[:, dcol:dcol + 512], in0=t1[:], in1=t2[:], op=ALU.add), reads=["t1", "t2"], writes=["qk"])

                        for i in range(len(groups) + 1):
                            if i < len(groups):
                                stage_a(groups[i])
                            if i >= 1:
                                stage_b(groups[i - 1])

                    for hp in range(3):
                        ws = wtile(384 + 128 * hp)
                        roles.update({"pA": [0, 1, 2, 3, 4, 5], "pT": [6, 7]})
                        rope_seq(ws, list(range(8)), kT, 0)
                        chk(2.1)
                        proj_s(ws, 384 + 128 * hp)
                        chk(2.2)
                        ws = wtile(128 * hp)
                        rope_seq(ws, [4, 5, 6, 7], qT, 2048)
                        proj_s(ws, 128 * hp)
                        ws = wtile(768 + 128 * hp)
                        for g in range(8):
                            pa = nxt("pA")
                            proj(ws, g, pa)
                            cp(vT[:, 512 * g:512 * g + 512], PA(pa)[:], ["pB%d" % pa], ["vT"])
                        proj_s(ws, 768 + 128 * hp)
                        chk(2.4)
                        ws = wtile(1152 + 128 * hp)
                        for g in range(4, 8):
                            pa = nxt("pA")
                            proj(ws, g, pa)
                            S.op("act", lambda e, g=g, pa=pa: e.activation(out=mixT[:, hp, 512 * (g - 4):512 * (g - 4) + 512], in_=PA(pa)[:], func=AF.Silu),
                                 reads=["pB%d" % pa], writes=["mix%d" % hp])
                        proj_s(ws, 1152 + 128 * hp)
                        chk(2.5)

                        tiles = []
                        for t in range(15, 32):
                            tiles.append((t - 15, slice(128 * t, 128 * t + 128)))
                        for r in range(4):
                            for J in range(3, 8):
                                tiles.append((17 + 5 * r + (J - 3), sl(512 * J + r, 128, 4)))
                        for r in range(16):
                            for J in range(2):
                                tiles.append((37 + 2 * r + J, sl(2048 * J + r, 128, 16)))
                        for b0 in range(0, 69, 8):
                            batch = tiles[b0:b0 + 8]
                            n = len(batch)
                            ps_ = nxt("pT")
                            for k_, (ti, tsl) in enumerate(batch):
                                S.op("pe", lambda e, k_=k_, tsl=tsl: e.transpose(PT(ps_)[:, 128 * k_:128 * k_ + 128], vT[:, tsl], ident[:]),
                                     reads=["vT", "ident"], writes=["pB%d" % ps_])
                            i0 = batch[0][0]
                            cp(vx[:, i0:i0 + n, 0:3:2, :], PT(ps_)[:, 0:128 * n].rearrange("p (t b d) -> p t b d", b=2, d=64),
                               ["pB%d" % ps_], ["vx"])

                        chk(2.6)
                        chk(2.7)
                        for hh in range(2):
                            S.dma("pool", vp.rearrange("(t p) (h d) -> p t h d", p=128, d=64)[:, :, 2 * hp + hh, :], vx[:, 1:17, 2 * hh, :], reads=["vx"])

                        S.op("pool", lambda e: e.tensor_copy(out=vT[:, 0:2048].rearrange("p (r j) -> p r j", r=4), in_=qT[:, :].rearrange("p (j r) -> p r j", r=4)),
                             reads=["qk", "vT"], writes=["vT"])
                        S.op("pool", lambda e: e.tensor_copy(out=vT[:, 2048:4096].rearrange("p (r j) -> p r j", r=16), in_=qT[:, :].rearrange("p (j r) -> p r j", r=16)),
                             reads=["qk", "vT"], writes=["vT"])
                        chk(3)
                        roles.update({"pS": [0, 1, 2, 3], "pO": [4, 5, 6, 7]})
                        for hh in range(2):
                            r0 = 64 * hh
                            rows = slice(r0, r0 + 64)
                            zrows = slice(64 - r0, 128 - r0)
                            vcol = slice(64 * hh, 64 * hh + 128)

                            banks = []
                            for ob in range(4):
                                for b2 in range(2):
                                    t = 16 + 4 * ob + 2 * b2
                                    base = 256 * b2
                                    units = [
                                        (slice(128 * (t - 1), 128 * t), t - 1 - 15, qT[rows, 128 * (t - 16):128 * (t - 15)], 128, 0, base),
                                        (slice(128 * t, 128 * (t + 1)), t - 15, qT[rows, 128 * (t - 16):128 * (t - 14)], 256, 128, base),
                                        (slice(128 * (t + 1), 128 * (t + 2)), t + 1 - 15, qT[rows, 128 * (t - 15):128 * (t - 14)], 128, 384, base + 128),
                                    ]
                                    ev = (acc[:, 512 * ob:512 * ob + 512], None, True) if b2 == 1 else None
                                    banks.append((units, ob, b2 == 0, ev))
                            for r in range(4):
                                for b2 in range(2):
                                    J = 4 + 2 * b2
                                    base = 256 * b2
                                    vi = lambda Jk, r=r: 17 + 5 * r + (Jk - 3)
                                    units = [
                                        (sl(512 * (J - 1) + r, 128, 4), vi(J - 1), vT[rows, 512 * r + 128 * (J - 4):512 * r + 128 * (J - 3)], 128, 0, base),
                                        (sl(512 * J + r, 128, 4), vi(J), vT[rows, 512 * r + 128 * (J - 4):512 * r + 128 * (J - 2)], 256, 128, base),
                                        (sl(512 * (J + 1) + r, 128, 4), vi(J + 1), vT[rows, 512 * r + 128 * (J - 3):512 * r + 128 * (J - 2)], 128, 384, base + 128),
                                    ]
                                    ev = (acc[:, sl(r, 512, 4)], None, False) if b2 == 1 else None
                                    banks.append((units, 4 + r, b2 == 0, ev))
                            for ob in range(4):
                                for b2 in range(2):
                                    units = []
                                    for k_ in range(2):
                                        r = 4 * ob + 2 * b2 + k_
                                        oc0 = 128 * (2 * b2 + k_)
                                        for J in range(2):
                                            units.append((sl(2048 * J + r, 128, 16), 37 + 2 * r + J, vT[rows, 2048 + 128 * r:2048 + 128 * r + 128], 128, 256 * k_ + 128 * J, oc0))
                                    ev = (acc[:].rearrange("p (j r) -> p r j", r=16)[:, 4 * ob:4 * ob + 4, :], "rj", False) if b2 == 1 else None
                                    banks.append((units, 8 + ob, b2 == 0, ev))

                            LAG = 3
                            bst = {}
                            ost = {}

                            def emit_s(i):
                                units = banks[i][0]
                                s_ = nxt("pS")
                                eb = nxt("eb", 4)
                                bst[i] = (s_, eb)
                                for (ksl, ti, qsl, n, c0, oc0) in units:
                                    S.op("pe", lambda e, ksl=ksl, qsl=qsl, n=n, c0=c0: e.matmul(PA(s_)[:, c0:c0 + n], kT[rows, ksl], qsl, start=True, stop=True),
                                         reads=(["qk", "vT"] if i >= 8 else ["qk"]), writes=["pB%d" % s_])
                                S.op("act", lambda e: e.activation(out=Eb[eb][:], in_=PA(s_)[:], func=AF.Exp, scale=0.125), reads=["pB%d" % s_], writes=["Eb%d" % eb])
                                S.op("dve", lambda e: e.tensor_tensor(out=Eb[eb][:], in0=Eb[eb][:], in1=mask4[:], op=ALU.mult),
                                     reads=["Eb%d" % eb, "mask4"], writes=["Eb%d" % eb])

                            def emit_pv(i):
                                units, obid, first, ev = banks[i]
                                s_, eb = bst[i]
                                if first:
                                    ost[obid] = nxt("pO")
                                oslot = ost[obid]
                                for (ksl, ti, qsl, n, c0, oc0) in units:
                                    S.op("pe", lambda e, ti=ti, n=n, c0=c0, oc0=oc0, first=first: e.matmul(PA(oslot)[:, oc0:oc0 + n], vxf[:, ti, vcol], Eb[eb][:, c0:c0 + n],
                                                                                                        start=first, stop=False, skip_group_check=True),
                                         reads=["vx", "vx1", "Eb%d" % eb], writes=["pB%d" % oslot])
                                    first = False
                                if ev is not None:
                                    acc_ap, mode, first_dil = ev
                                    in_ap = PA(oslot)[:] if mode is None else PA(oslot)[:].rearrange("p (r j) -> p r j", j=128)
                                    if first_dil:
                                        S.op("act", lambda e: e.copy(acc_ap, in_ap), reads=["pB%d" % oslot], writes=["acc"])
                                    else:
                                        S.op("dve", lambda e: e.tensor_tensor(out=acc_ap, in0=in_ap, in1=acc_ap, op=ALU.add), reads=["pB%d" % oslot, "acc"], writes=["acc"])

                            for i in range(len(banks) + LAG):
                                if i < len(banks):
                                    emit_s(i)
                                if i >= LAG:
                                    emit_pv(i - LAG)
                            for pc in range(8):
                                cs = slice(256 * pc, 256 * pc + 256)
                                S.op("act", lambda e, cs=cs: e.activation(out=rz[rows, :], in_=acc[zrows, cs], func=AF.Ln), reads=["acc"], writes=["rz"])
                                S.op("act", lambda e: e.activation(out=rz[rows, :], in_=rz[rows, :], func=AF.Exp, scale=-1.0), reads=["rz"], writes=["rz"])
                                S.op("dve", lambda e, cs=cs: e.tensor_tensor(out=tmpn[rows, :], in0=acc[rows, cs], in1=rz[rows, :], op=ALU.mult),
                                     reads=["acc", "rz"], writes=["tmpn"])
                                S.op("pool", lambda e, cs=cs: e.tensor_tensor(out=mixT[rows, hp, cs], in0=tmpn[rows, :], in1=mixT[rows, hp, cs], op=ALU.mult),
                                     reads=["tmpn", "mix%d" % hp], writes=["mix%d" % hp])
                        roles.update({"pT": [6, 7]})
                        for b0 in range(0, 16, 2):
                            ps_ = nxt("pT")
                            for k_ in range(2):
                                t = 16 + b0 + k_
                                S.op("pe", lambda e, k_=k_, t=t: e.transpose(PT(ps_)[:, 128 * k_:128 * k_ + 128], kT[:, 128 * t:128 * t + 128], ident[:]),
                                     reads=["qk", "ident"], writes=["pB%d" % ps_])
                            cp(ko32[:].rearrange("p t f -> p (t f)"), PT(ps_)[:, 0:256], ["pB%d" % ps_], ["ko32"])
                            S.dma("sp", kp[128 * b0:128 * b0 + 256, 128 * hp:128 * hp + 128].rearrange("(t p) f -> p t f", p=128), ko32[:], reads=["ko32"])
                    S.barrier()

            roles.update({"pT": [0, 1], "pA": [2, 3], "pS": [4, 5], "pO": [6, 7]})
            chk(4)
            with ExitStack() as p3:
                ident32 = sbt(p3, "ident32", [128, 128], F32)
                ones32 = sbt(p3, "ones32", [128, 128], F32)
                S.op("pool", lambda e: e.memset(ones32[:], 1.0), writes=["ones32"])
                S.op("pool", lambda e: e.affine_select(out=ident32[:], in_=ones32[:], pattern=[[1, 128]], compare_op=ALU.is_equal,
                                                       fill=0.0, base=0, channel_multiplier=-1), reads=["ones32"], writes=["ident32"])
                qmT = sbt(p3, "qmT", [128, 2, 2048], BF16)
                sT = sbt(p3, "sT", [128, 19, 32], BF16)
                vnew = sbt(p3, "vnew", [8, 4, 384], BF16)
                qsz = sbt(p3, "qsz", [128, 6, 32], BF16)
                qmz = sbt(p3, "qmz", [128, 4, 32], BF16)
                rz = [sbt(p3, "rz2_%d" % i, [128, 512], F32) for i in range(2)]
                tmpn = [sbt(p3, "tmpn2_%d" % i, [128, 512], F32) for i in range(2)]
                Eb = [sbt(p3, "Ec%d" % i, [128, 512], BF16) for i in range(4)]

                with ExitStack() as p3c:
                    uT = sbt(p3c, "uT", [128, 3, 2080], BF16)
                    u32 = sbt(p3c, "u32", [128, 3, 32], F32)
                    sg = [sbt(p3c, "sg%d" % i, [128, 512], F32) for i in range(2)]
                    Dg = sbt(p3c, "Dg", [128, 31, 128], BF16)
                    wdw = sbt(p3c, "wdw", [128, 3, 31], F32)
                    wpw = sbt(p3c, "wpw", [128, 3, 384], BF16)
                    cT = sbt(p3c, "cT", [128, 3, 2048], F32)
                    cTs = sbt(p3c, "cTs", [128, 3, 32], F32)
                    cb = sbt(p3c, "cb", [128, 3, 512], BF16)
                    csq = sbt(p3c, "csq", [128, 3, 512], BF16)
                    mean = sbt(p3c, "mean", [128, 512], F32)
                    msq = sbt(p3c, "msq", [128, 512], F32)
                    var = sbt(p3c, "var", [128, 512], F32)
                    rs = sbt(p3c, "rs", [128, 512], F32)
                    dd = [sbt(p3c, "dd%d" % i, [128, 512], F32) for i in range(2)]
                    sw = sbt(p3c, "sw", [128, 3, 512], BF16)
                    ucT = sbt(p3c, "ucT", [128, 3, 4, 38], BF16)
                    scb = sbt(p3c, "scb", [30, 4, 384], BF16)
                    rcs_sb = sbt(p3c, "rcs_sb", [32, 49], F32)
                    sa = sbt(p3c, "sa", [32, 48], F32)
                    sb2 = sbt(p3c, "sb2", [32, 48], F32)
                    ski = sbt(p3c, "ski", [32, 48], I32)
                    skf = sbt(p3c, "skf", [32, 48], F32)
                    cos6 = sbt(p3c, "cos6", [32, 48], F32)
                    sin6 = sbt(p3c, "sin6", [32, 48], F32)
                    tA = sbt(p3c, "tA", [32, 48], F32)
                    tB = sbt(p3c, "tB", [32, 48], F32)
                    tC = sbt(p3c, "tC", [32, 48], F32)
                    tD = sbt(p3c, "tD", [32, 48], F32)
                    us32 = sbt(p3c, "us32", [32, 384], F32)
                    sgs = sbt(p3c, "sgs", [32, 384], F32)
                    tokb = sbt(p3c, "tokb", [32, 2816], BF16)
                    cvo = sbt(p3c, "cvo", [32, 384], F32)

                    roles.update({"pT": [0, 1], "pA": [2, 3, 4, 5, 6, 7]})
                    S.dma("sp", wdw[:], w_dwT[:], writes=["wdw"])
                    S.dma("pool", wpw[:], w_pw2.rearrange("(c p) n -> p c n", p=128), writes=["wpw"])
                    S.dma("sp", rcs_sb[:], rcs[:], writes=["rcs_sb"])
                    S.dma("pool", scb[:], sc.rearrange("b r n -> r b n"), writes=["scb"])

                    for cc in range(3):
                        wa = wtile(1536 + 128 * cc)
                        wb = wtile(1920 + 128 * cc)
                        for g in [3.5, 4, 5, 6, 7]:
                            pa, pb_ = nxt("pA"), nxt("pA")
                            if g == 3.5:
                                ntok, rf, dcol = 32, (lambda c: hTh32[:, c, :]), 0
                                proj(wa, None, pa, ntok, rf)
                                proj(wb, None, pb_, ntok, rf)
                            else:
                                ntok, dcol = 512, 32 + 512 * (g - 4)
                                proj(wa, g, pa)
                                proj(wb, g, pb_)
                            sgi = nxt("cp")
                            S.op("act", lambda e, pb_=pb_, ntok=ntok, sgi=sgi: e.activation(out=sg[sgi][:, 0:ntok], in_=PA(pb_)[:, 0:ntok], func=AF.Sigmoid),
                                 reads=["pB%d" % pb_], writes=["sg%d" % sgi])
                            S.op("dve", lambda e, pa=pa, ntok=ntok, dcol=dcol, sgi=sgi: e.tensor_tensor(out=uT[:, cc, dcol:dcol + ntok], in0=PA(pa)[:, 0:ntok],
                                                                                                    in1=sg[sgi][:, 0:ntok], op=ALU.mult),
                                 reads=["pB%d" % pa, "sg%d" % sgi], writes=["uT"])
                            if g == 7:
                                S.op("dve", lambda e, pa=pa, sgi=sgi: e.tensor_tensor(out=u32[:, cc, :], in0=PA(pa)[:, 480:512], in1=sg[sgi][:, 480:512], op=ALU.mult),
                                     reads=["pB%d" % pa, "sg%d" % sgi], writes=["u32"])
                        proj_s(wa, 1536 + 128 * cc)
                        proj_s(wb, 1920 + 128 * cc)
                        wg = wtile(2304 + 128 * cc)
                        for g in range(4, 8):
                            pa = nxt("pA")
                            proj(wg, g, pa)
                            S.op("act", lambda e, g=g, pa=pa: e.activation(out=mixT[:, 3 + cc, 512 * (g - 4):512 * (g - 4) + 512], in_=PA(pa)[:], func=AF.Silu),
                                 reads=["pB%d" % pa], writes=["mix%d" % (3 + cc)])
                        proj_s(wg, 2304 + 128 * cc)
                    for mp in range(2):
                        wq = wtile(2688 + 128 * mp)
                        for g in range(4, 8):
                            pa = nxt("pA")
                            proj(wq, g, pa)
                            cp(qmT[:, mp, 512 * (g - 4):512 * (g - 4) + 512], PA(pa)[:], ["pB%d" % pa], ["qmT"])
                        proj_s(wq, 2688 + 128 * mp)
                        wg = wtile(2944 + 128 * mp)
                        for g in range(4, 8):
                            pa = nxt("pA")
                            proj(wg, g, pa)
                            S.op("act", lambda e, g=g, pa=pa: e.activation(out=mixT[:, 6 + mp, 512 * (g - 4):512 * (g - 4) + 512], in_=PA(pa)[:], func=AF.Silu),
                                 reads=["pB%d" % pa], writes=["mix%d" % (6 + mp)])
                        proj_s(wg, 2944 + 128 * mp)
                    pa = nxt("pA")
                    for cc in range(3):
                        S.op("pe", lambda e, cc=cc: e.transpose(PA(pa)[0:32, 128 * cc:128 * cc + 128], u32[:, cc, :], ident32[:]),
                             reads=["u32", "ident32"], writes=["pB%d" % pa])
                    cp(cvo[:], PA(pa)[0:32, 0:384], ["pB%d" % pa], ["cvo"])
                    S.dma("sp", cvp[:, :], cvo[2:32, :], reads=["cvo"])
                    chk(4.1)

                    S.op("dve", lambda e: e.tensor_scalar(out=sa[:], in0=rcs_sb[:, 1:49], scalar1=rcs_sb[:, 0:1], scalar2=None, op0=ALU.mult),
                         reads=["rcs_sb"], writes=["sa"])
                    for which in range(2):
                        src, sname = sa, "sa"
                        if which == 1:
                            S.op("dve", lambda e: e.tensor_scalar(out=sb2[:], in0=sa[:], scalar1=math.pi / 2, scalar2=None, op0=ALU.add), reads=["sa"], writes=["sb2"])
                            src, sname = sb2, "sb2"
                        S.op("dve", lambda e, src=src: e.tensor_scalar(out=ski[:], in0=src[:], scalar1=1.0 / TWO_PI, scalar2=None, op0=ALU.mult), reads=[sname], writes=["ski"])
                        S.op("dve", lambda e: e.tensor_copy(out=skf[:], in_=ski[:]), reads=["ski"], writes=["skf"])
                        S.op("dve", lambda e, src=src: e.scalar_tensor_tensor(out=src[:], in0=skf[:], scalar=-C1, in1=src[:], op0=ALU.mult, op1=ALU.add), reads=["skf", sname], writes=[sname])
                        S.op("dve", lambda e, src=src: e.scalar_tensor_tensor(out=src[:], in0=skf[:], scalar=-C2, in1=src[:], op0=ALU.mult, op1=ALU.add), reads=["skf", sname], writes=[sname])
                        S.op("dve", lambda e, src=src: e.tensor_scalar(out=src[:], in0=src[:], scalar1=-PI_CL, scalar2=PI_CL, op0=ALU.max, op1=ALU.min), reads=[sname], writes=[sname])
                        dstt, dn = (sin6, "sin6") if which == 0 else (cos6, "cos6")
                        S.op("act", lambda e, src=src, dstt=dstt: e.activation(out=dstt[:], in_=src[:], func=AF.Sin), reads=[sname], writes=[dn])
                    c6 = cos6[:].rearrange("p (h d) -> p h d", d=8)
                    s6 = sin6[:].rearrange("p (h d) -> p h d", d=8)
                    v3 = lambda t: t[:].rearrange("p (h d) -> p h d", d=8)
                    for base in (0, 384):
                        qv = ts32[:, base:base + 384].rearrange("p (h d) -> p h d", d=64)
                        x1, x2 = qv[:, :, 0:8], qv[:, :, 8:16]
                        S.op("dve", lambda e, x1=x1: e.tensor_tensor(out=v3(tA), in0=x1, in1=c6, op=ALU.mult), reads=["ts32", "cos6"], writes=["tA"])
                        S.op("dve", lambda e, x2=x2: e.tensor_tensor(out=v3(tB), in0=x2, in1=s6, op=ALU.mult), reads=["ts32", "sin6"], writes=["tB"])
                        S.op("dve", lambda e, x2=x2: e.tensor_tensor(out=v3(tC), in0=x2, in1=c6, op=ALU.mult), reads=["ts32", "cos6"], writes=["tC"])
                        S.op("dve", lambda e, x1=x1: e.tensor_tensor(out=v3(tD), in0=x1, in1=s6, op=ALU.mult), reads=["ts32", "sin6"], writes=["tD"])
                        S.op("dve", lambda e, x1=x1: e.tensor_tensor(out=x1, in0=v3(tA), in1=v3(tB), op=ALU.subtract), reads=["tA", "tB"], writes=["ts32"])
                        S.op("dve", lambda e, x2=x2: e.tensor_tensor(out=x2, in0=v3(tC), in1=v3(tD), op=ALU.add), reads=["tC", "tD"], writes=["ts32"])
                    S.op("act", lambda e: e.activation(out=sgs[:], in_=ts32[:, 1920:2304], func=AF.Sigmoid), reads=["ts32"], writes=["sgs"])
                    S.op("dve", lambda e: e.tensor_tensor(out=us32[:], in0=ts32[:, 1536:1920], in1=sgs[:], op=ALU.mult), reads=["ts32", "sgs"], writes=["us32"])
                    for bl in range(4):
                        S.dma("sp", ksn[bl, 2040:2048, :], ts32[8 * bl:8 * bl + 8, 384:768], reads=["ts32"])
                        S.dma("sp", vsn[bl, 2040:2048, :], ts32[8 * bl:8 * bl + 8, 768:1152], reads=["ts32"])
                        S.dma("sp", csn[bl, 22:30, :], us32[8 * bl:8 * bl + 8, :], reads=["us32"])
                    S.op("dve", lambda e: e.tensor_copy(out=tokb[:, 0:768], in_=ts32[:, 0:768]), reads=["ts32"], writes=["tokb"])
                    S.op("dve", lambda e: e.tensor_copy(out=tokb[:, 1024:1408], in_=us32[:]), reads=["us32", "tokb"], writes=["tokb"])
                    S.op("dve", lambda e: e.tensor_copy(out=tokb[:, 768:1024], in_=ts32[:, 2688:2944]), reads=["ts32", "tokb"], writes=["tokb"])
                    S.op("act", lambda e: e.activation(out=tokb[:, 1408:1792], in_=ts32[:, 1152:1536], func=AF.Silu), reads=["ts32", "tokb"], writes=["tokb"])
                    S.op("act", lambda e: e.activation(out=tokb[:, 1792:2176], in_=ts32[:, 2304:2688], func=AF.Silu), reads=["ts32", "tokb"], writes=["tokb"])
                    S.op("act", lambda e: e.activation(out=tokb[:, 2176:2432], in_=ts32[:, 2944:3200], func=AF.Silu), reads=["ts32", "tokb"], writes=["tokb"])
                    S.op("dve", lambda e: e.tensor_copy(out=tokb[:, 2432:2816], in_=ts32[:, 768:1152]), reads=["ts32", "tokb"], writes=["tokb"])
                    ps_ = nxt("pT")
                    for k_ in range(19):
                        S.op("pe", lambda e, k_=k_: e.transpose(PT(ps_)[:, 32 * k_:32 * k_ + 32], tokb[:, 128 * k_:128 * k_ + 128], ident[0:32, 0:32]),
                             reads=["tokb", "ident"], writes=["pB%d" % ps_])
                    cp(sT[:].rearrange("p k t -> p (k t)"), PT(ps_)[:, 0:608], ["pB%d" % ps_], ["sT"])
                    S.op("pool", lambda e: e.memset(qsz[:], 0.0), writes=["qsz"])
                    S.op("pool", lambda e: e.memset(qmz[:], 0.0), writes=["qmz"])
                    for h_ in range(6):
                        hr_ = slice(64 * (h_ % 2), 64 * (h_ % 2) + 64)
                        S.op("pool", lambda e, h_=h_, hr_=hr_: e.tensor_copy(out=qsz[hr_, h_, :], in_=sT[hr_, h_ // 2, :]), reads=["sT", "qsz"], writes=["qsz"])
                    for h_ in range(4):
                        hr_ = slice(64 * (h_ % 2), 64 * (h_ % 2) + 64)
                        S.op("pool", lambda e, h_=h_, hr_=hr_: e.tensor_copy(out=qmz[hr_, h_, :], in_=sT[hr_, 6 + h_ // 2, :]), reads=["sT", "qmz"], writes=["qmz"])
                    pa = nxt("pA")
                    for bl in range(4):
                        pass
                    for bl in range(4):
                        pa = nxt("pA")
                        S.op("pe", lambda e, bl=bl, pa=pa: e.matmul(PA(pa)[0:8, 0:384], ident[0:32, 8 * bl:8 * bl + 8], tokb[:, 2432:2816], start=True, stop=True),
                             reads=["tokb", "ident"], writes=["pB%d" % pa])
                        cp(vnew[:, bl, :], PA(pa)[0:8, 0:384], ["pB%d" % pa], ["vnew"])
                    ps_ = nxt("pT")
                    for cc in range(3):
                        for bl in range(4):
                            k_ = 4 * cc + bl
                            S.op("pe", lambda e, k_=k_, cc=cc, bl=bl: e.transpose(PT(ps_)[:, 32 * k_:32 * k_ + 30], scb[:, bl, 128 * cc:128 * cc + 128], ident[0:30, 0:30]),
                                 reads=["scb", "ident"], writes=["pB%d" % ps_])
                    cp(ucT[:].rearrange("p c b t -> p (c b) t")[:, :, 0:30], PT(ps_)[:, 0:384].rearrange("p (k t) -> p k t", t=32)[:, :, 0:30],
                       ["pB%d" % ps_], ["ucT"])
                    S.op("dve", lambda e: e.tensor_copy(out=ucT[:, :, :, 30:38], in_=sT[:, 8:11, :].rearrange("p c (b t) -> p c b t", t=8)),
                         reads=["sT", "ucT"], writes=["ucT"])
                    chk(4.2)

                    for cc in range(3):
                        for j in range(31):
                            S.op("dve", lambda e, j=j: e.tensor_scalar(out=Dg[:, j, :], in0=ident[:], scalar1=wdw[:, cc, j:j + 1], scalar2=None, op0=ALU.mult),
                                 reads=["ident", "wdw", "Dg"], writes=["Dg"])
                        for g in range(4):
                            pa = nxt("pA")
                            for j in range(31):
                                S.op("pe", lambda e, j=j, g=g, pa=pa: e.matmul(PA(pa)[:], Dg[:, j, :], uT[:, cc, 512 * g + j + 2:512 * g + j + 2 + 512],
                                                                               start=(j == 0), stop=(j == 30)),
                                     reads=["Dg", "uT"], writes=["pB%d" % pa])
                            S.op("act", lambda e, g=g, pa=pa: e.activation(out=cT[:, cc, 512 * g:512 * g + 512], in_=PA(pa)[:], func=AF.Identity, bias=vec_sb[:, 0, cc:cc + 1]),
                                 reads=["pB%d" % pa, "vec_sb"], writes=["cT"])
                        pa = nxt("pA")
                        for j in range(31):
                            S.op("pe", lambda e, j=j, pa=pa: e.matmul(PA(pa)[:, 0:32], Dg[:, j, :], ucT[:, cc, :, j:j + 8], start=(j == 0), stop=(j == 30)),
                                 reads=["Dg", "ucT"], writes=["pB%d" % pa])
                        S.op("act", lambda e, pa=pa: e.activation(out=cTs[:, cc, :], in_=PA(pa)[:, 0:32], func=AF.Identity, bias=vec_sb[:, 0, cc:cc + 1]),
                             reads=["pB%d" % pa, "vec_sb"], writes=["cTs"])
                    chk(4.3)

                    mean2 = [mean, tmpn[1]]
                    rs2_ = [rs, rz[1]]

                    def conf_front(k, cap, cname, N):
                        mn, rsx = mean2[k % 2], rs2_[k % 2]
                        mname, rname = "mean%d" % (k % 2), "rs%d" % (k % 2)
                        for cc in range(3):
                            S.op("pool", lambda e, cc=cc: e.tensor_copy(out=cb[:, cc, 0:N], in_=cap(cc)), reads=[cname, "cb"], writes=["cb"])
                            S.op("act", lambda e, cc=cc: e.activation(out=csq[:, cc, 0:N], in_=cap(cc), func=AF.Square), reads=[cname, "csq"], writes=["csq"])
                        pm, pq = nxt("pA"), nxt("pA")
                        for cc in range(3):
                            S.op("pe", lambda e, cc=cc: e.matmul(PA(pm)[:, 0:N], onesb[:], cb[:, cc, 0:N], start=(cc == 0), stop=(cc == 2)),
                                 reads=["onesb", "cb"], writes=["pB%d" % pm])
                        for cc in range(3):
                            S.op("pe", lambda e, cc=cc: e.matmul(PA(pq)[:, 0:N], onesb[:], csq[:, cc, 0:N], start=(cc == 0), stop=(cc == 2)),
                                 reads=["onesb", "csq"], writes=["pB%d" % pq])
                        S.op("dve", lambda e: e.tensor_scalar(out=mn[:, 0:N], in0=PA(pm)[:, 0:N], scalar1=1.0 / 384, scalar2=None, op0=ALU.mult),
                             reads=["pB%d" % pm], writes=[mname])
                        S.op("dve", lambda e: e.tensor_tensor(out=msq[:, 0:N], in0=mn[:, 0:N], in1=mn[:, 0:N], op=ALU.mult), reads=[mname], writes=["msq"])
                        S.op("dve", lambda e: e.scalar_tensor_tensor(out=var[:, 0:N], in0=PA(pq)[:, 0:N], scalar=1.0 / 384, in1=msq[:, 0:N], op0=ALU.mult, op1=ALU.subtract),
                             reads=["pB%d" % pq, "msq"], writes=["var"])
                        S.op("act", lambda e: e.activation(out=var[:, 0:N], in_=var[:, 0:N], func=AF.Ln, bias=epst[:, 0:1]), reads=["var", "epst"], writes=["var"])
                        S.op("act", lambda e: e.activation(out=rsx[:, 0:N], in_=var[:, 0:N], func=AF.Exp, scale=-0.5), reads=["var"], writes=[rname])

                    def conf_back(k, cap, cname, N, out_fn):
                        mn, rsx = mean2[k % 2], rs2_[k % 2]
                        mname, rname = "mean%d" % (k % 2), "rs%d" % (k % 2)
                        for cc in range(3):
                            di = nxt("cp")
                            S.op("dve", lambda e, cc=cc, di=di: e.tensor_tensor(out=dd[di][:, 0:N], in0=cap(cc), in1=mn[:, 0:N], op=ALU.subtract),
                                 reads=[cname, mname], writes=["dd%d" % di])
                            S.op("dve", lambda e, di=di: e.tensor_tensor(out=dd[di][:, 0:N], in0=dd[di][:, 0:N], in1=rsx[:, 0:N], op=ALU.mult),
                                 reads=["dd%d" % di, rname], writes=["dd%d" % di])
                            S.op("act", lambda e, cc=cc, di=di: e.activation(out=sw[:, cc, 0:N], in_=dd[di][:, 0:N], func=AF.Silu,
                                                                             scale=vec_sb[:, 1, cc:cc + 1], bias=vec_sb[:, 2, cc:cc + 1]),
                                 reads=["dd%d" % di, "vec_sb", "sw"], writes=["sw"])
                        for fo in range(3):
                            po = nxt("pA")
                            for cc in range(3):
                                S.op("pe", lambda e, cc=cc, fo=fo, po=po: e.matmul(PA(po)[:, 0:N], wpw[:, cc, 128 * fo:128 * fo + 128], sw[:, cc, 0:N],
                                                                                   start=(cc == 0), stop=(cc == 2)),
                                     reads=["wpw", "sw"], writes=["pB%d" % po])
                            out_fn(fo, po)

                    def mk_out_p(g):
                        def out_p(fo, po):
                            dst = mixT[:, 3 + fo, 512 * g:512 * g + 512]
                            S.op("dve", lambda e: e.scalar_tensor_tensor(out=dst, in0=PA(po)[:], scalar=vec_sb[:, 3, fo:fo + 1], in1=dst, op0=ALU.add, op1=ALU.mult),
                                 reads=["pB%d" % po, "vec_sb", "mix%d" % (3 + fo)], writes=["mix%d" % (3 + fo)])
                        return out_p

                    def out_s(fo, po):
                        S.op("dve", lambda e: e.scalar_tensor_tensor(out=mixTs[:, 3 + fo, :], in0=PA(po)[:, 0:32], scalar=vec_sb[:, 3, fo:fo + 1], in1=sT[:, 14 + fo, :],
                                                                     op0=ALU.add, op1=ALU.mult),
                             reads=["pB%d" % po, "vec_sb", "sT"], writes=["mixTs"])

                    cjobs = [((lambda cc, g=g: cT[:, cc, 512 * g:512 * g + 512]), "cT", 512, mk_out_p(g)) for g in range(4)]
                    cjobs.append(((lambda cc: cTs[:, cc, :]), "cTs", 32, out_s))
                    for k in range(len(cjobs) + 1):
                        if k < len(cjobs):
                            conf_front(k, cjobs[k][0], cjobs[k][1], cjobs[k][2])
                        if k >= 1:
                            j = cjobs[k - 1]
                            conf_back(k - 1, j[0], j[1], j[2], j[3])
                    roles.update({"pT": [0, 1], "pA": [2, 3], "pS": [4, 5], "pO": [6, 7]})
                    S.barrier()
                chk(5)

                with ExitStack() as p4:
                    wm = sbt(p4, "wm", [128, 8, 512], BF16)
                    kmT = sbt(p4, "kmT", [128, 2, 256], BF16)
                    vxm = sbt(p4, "vxm", [128, 2, 2, 3, 64], BF16)
                    vxmf = vxm[:].rearrange("p t m b d -> p t m (b d)")
                    mkv32 = sbt(p4, "mkv32", [128, 2, 512], F32)
                    cmkb = sbt(p4, "cmkb", [128, 2, 256], BF16)
                    cmvb = sbt(p4, "cmvb", [128, 2, 256], BF16)
                    kmsT = sbt(p4, "kmsT", [128, 2, 256], BF16)
                    Es = sbt(p4, "Es", [128, 16], BF16)
                    S.dma("pool", wm[:], w_mkv.rearrange("(c p) n -> p c n", p=128), writes=["wm"])
                    S.op("pool", lambda e: e.memset(vxm[:, :, :, 1, :], 1.0), writes=["vxm1"])
                    for mt in range(2):
                        pa = nxt("pA")
                        for c in range(8):
                            S.op("pe", lambda e, c=c, mt=mt, pa=pa: e.matmul(PA(pa)[:], hTm[:, c, 128 * mt:128 * mt + 128], wm[:, c, :], start=(c == 0), stop=(c == 7)),
                                 reads=["wm", "hT"], writes=["pB%d" % pa])
                        cp(mkv32[:, mt, :], PA(pa)[:], ["pB%d" % pa], ["mkv32"])
                        S.dma("sp", mkp[128 * mt:128 * mt + 128, :], mkv32[:, mt, 0:256], reads=["mkv32"])
                        S.dma("sp", mvp[128 * mt:128 * mt + 128, :], mkv32[:, mt, 256:512], reads=["mkv32"])
                        for mp in range(2):
                            S.op("pool", lambda e, mt=mt, mp=mp: e.tensor_copy(out=vxm[:, mt, mp, 0:3:2, :],
                                                                               in_=mkv32[:, mt, 256 + 128 * mp:256 + 128 * mp + 128].rearrange("p (b d) -> p b d", d=64)),
                                 reads=["mkv32"], writes=["vxm"])
                    for mp in range(2):
                        pa = nxt("pA")
                        for c in range(8):
                            S.op("pe", lambda e, c=c, mp=mp, pa=pa: e.matmul(PA(pa)[:, 0:256], wm[:, c, 128 * mp:128 * mp + 128], hTm[:, c, :], start=(c == 0), stop=(c == 7)),
                                 reads=["wm", "hT"], writes=["pB%d" % pa])
                        cp(kmT[:, mp, :], PA(pa)[:, 0:256], ["pB%d" % pa], ["kmT"])
                    roles.update({"pS": [0, 1, 2, 3], "pO": [4, 5, 6, 7]})
                    its = [(mh, g) for mh in range(4) for g in range(4)]
                    mst = {}

                    def mem_s(i):
                        mh, g = its[i]
                        mp, hh = mh // 2, mh % 2
                        rows = slice(64 * hh, 64 * hh + 64)
                        cs = slice(512 * g, 512 * g + 512)
                        sl_ = []
                        for kt in range(2):
                            s_ = nxt("pS")
                            eb = nxt("eb", 4)
                            S.op("pe", lambda e, kt=kt, s_=s_: e.matmul(PA(s_)[:], kmT[rows, mp, 128 * kt:128 * kt + 128], qmT[rows, mp, cs], start=True, stop=True),
                                 reads=["kmT", "qmT"], writes=["pB%d" % s_])
                            S.op("act", lambda e, s_=s_, eb=eb: e.activation(out=Eb[eb][:], in_=PA(s_)[:], func=AF.Exp, scale=0.125), reads=["pB%d" % s_], writes=["Ec%d" % eb])
                            sl_.append(eb)
                        mst[i] = sl_

                    def mem_pv(i):
                        mh, g = its[i]
                        mp, hh = mh // 2, mh % 2
                        r0 = 64 * hh
                        rows = slice(r0, r0 + 64)
                        zrows = slice(64 - r0, 128 - r0)
                        vcol = slice(64 * hh, 64 * hh + 128)
                        cs = slice(512 * g, 512 * g + 512)
                        oslot = nxt("pO")
                        for kt in range(2):
                            eb = mst[i][kt]
                            S.op("pe", lambda e, kt=kt, eb=eb: e.matmul(PA(oslot)[:], vxmf[:, kt, mp, vcol], Eb[eb][:], start=(kt == 0), stop=(kt == 1), skip_group_check=True),
                                 reads=["vxm", "vxm1", "Ec%d" % eb], writes=["pB%d" % oslot])
                        rzs = nxt("rzs")
                        S.op("act", lambda e: e.activation(out=rz[rzs][rows, :], in_=PA(oslot)[zrows, :], func=AF.Ln), reads=["pB%d" % oslot], writes=["rz2_%d" % rzs])
                        S.op("act", lambda e: e.activation(out=rz[rzs][rows, :], in_=rz[rzs][rows, :], func=AF.Exp, scale=-1.0), reads=["rz2_%d" % rzs], writes=["rz2_%d" % rzs])
                        S.op("dve", lambda e: e.tensor_tensor(out=tmpn[rzs][rows, :], in0=PA(oslot)[rows, :], in1=rz[rzs][rows, :], op=ALU.mult),
                             reads=["pB%d" % oslot, "rz2_%d" % rzs], writes=["tmpn2_%d" % rzs])
                        S.op("pool", lambda e: e.tensor_tensor(out=mixT[rows, 6 + mp, cs], in0=tmpn[rzs][rows, :], in1=mixT[rows, 6 + mp, cs], op=ALU.mult),
                             reads=["tmpn2_%d" % rzs, "mix%d" % (6 + mp)], writes=["mix%d" % (6 + mp)])

                    for i in range(len(its) + 1):
                        if i < len(its):
                            mem_s(i)
                        if i >= 1:
                            mem_pv(i - 1)

                    chk(5.5)
                    roles.update({"pT": [0, 1], "pS": [2, 3], "pO": [4, 5]})
                    Onm = sbt(p4, "Onm", [128, 2, 32], F32)
                    zsm = [sbt(p4, "zsm%d" % i, [128, 4, 8], F32) for i in range(2)]
                    Es2 = [sbt(p4, "Es2_%d" % i, [128, 64], BF16) for i in range(2)]
                    cmkb2 = [cmkb, sbt(p4, "cmkb_1", [128, 2, 256], BF16)]
                    cmvb2 = [cmvb, sbt(p4, "cmvb_1", [128, 2, 256], BF16)]
                    kmsT2 = [kmsT, sbt(p4, "kmsT_1", [128, 2, 256], BF16)]
                    def load_m(bl):
                        sl_ = bl % 2
                        S.dma("pool", cmkb2[sl_][:], cmk[bl].rearrange("(t p) n -> p t n", p=128), writes=["cmkb%d" % sl_])
                        S.dma("pool", cmvb2[sl_][:], cmv[bl].rearrange("(t p) n -> p t n", p=128), writes=["cmvb%d" % sl_])

                    load_m(0)
                    load_m(1)
                    for bl in range(4):
                        sl_ = bl % 2
                        ps_ = nxt("pT")
                        for kt in range(2):
                            for mp in range(2):
                                k_ = 2 * mp + kt
                                S.op("pe", lambda e, kt=kt, mp=mp, k_=k_: e.transpose(PT(ps_)[:, 128 * k_:128 * k_ + 128], cmkb2[sl_][:, kt, 128 * mp:128 * mp + 128], ident[:]),
                                     reads=["cmkb%d" % sl_, "ident"], writes=["pB%d" % ps_])
                        cp(kmsT2[sl_][:].rearrange("p m j -> p (m j)"), PT(ps_)[:, 0:512], ["pB%d" % ps_], ["kmsT%d" % sl_])
                        s_ = nxt("pS")
                        oslot = nxt("pO")
                        for mh in range(4):
                            mp, hh = mh // 2, mh % 2
                            rows = slice(64 * hh, 64 * hh + 64)
                            qs = qmz[:, mh, 8 * bl:8 * bl + 8]
                            for kt in range(2):
                                S.op("pe", lambda e, kt=kt, mh=mh, mp=mp, rows=rows, qs=qs: e.matmul(PA(s_)[:, 32 * kt + 8 * mh:32 * kt + 8 * mh + 8],
                                                                                                  kmsT2[sl_][:, mp, 128 * kt:128 * kt + 128], qs, start=True, stop=True),
                                     reads=["kmsT%d" % sl_, "qmz"], writes=["pB%d" % s_])
                        S.op("act", lambda e: e.activation(out=Es2[sl_][:], in_=PA(s_)[:, 0:64], func=AF.Exp, scale=0.125), reads=["pB%d" % s_], writes=["Es2_%d" % sl_])
                        for mh in range(4):
                            hh = mh % 2
                            rows = slice(64 * hh, 64 * hh + 64)
                            for kt in range(2):
                                S.op("pe", lambda e, kt=kt, mh=mh, rows=rows: e.matmul(PA(oslot)[rows, 8 * mh:8 * mh + 8], cmvb2[sl_][:, kt, 64 * mh:64 * mh + 64],
                                                                                      Es2[sl_][:, 32 * kt + 8 * mh:32 * kt + 8 * mh + 8],
                                                                                      start=(kt == 0), stop=(kt == 1), skip_group_check=True),
                                     reads=["cmvb%d" % sl_, "Es2_%d" % sl_], writes=["pB%d" % oslot])
                        for hf in range(2):
                            hr = slice(64 * hf, 64 * hf + 64)
                            for kt in range(2):
                                S.op("pe", lambda e, kt=kt, hr=hr: e.matmul(PA(oslot)[hr, 64:96], onesb[:, 0:64], Es2[sl_][:, 32 * kt:32 * kt + 32], start=(kt == 0), stop=(kt == 1), skip_group_check=True),
                                     reads=["onesb", "Es2_%d" % sl_], writes=["pB%d" % oslot])
                        for hf in range(2):
                            hr = slice(64 * hf, 64 * hf + 64)
                            S.op("dve", lambda e, hr=hr: e.reciprocal(out=zsm[sl_][hr].rearrange("p h q -> p (h q)"), in_=PA(oslot)[hr, 64:96]), reads=["pB%d" % oslot, "zsm%d" % sl_], writes=["zsm%d" % sl_])
                        for mh in range(4):
                            mp, hh = mh // 2, mh % 2
                            rows = slice(64 * hh, 64 * hh + 64)
                            S.op("dve", lambda e, mh=mh, mp=mp, rows=rows: e.tensor_tensor(out=Onm[rows, mp, 8 * bl:8 * bl + 8], in0=PA(oslot)[rows, 8 * mh:8 * mh + 8],
                                                                                         in1=zsm[sl_][rows, mh, :], op=ALU.mult),
                                 reads=["pB%d" % oslot, "zsm%d" % sl_, "Onm"], writes=["Onm"])
                        if bl + 2 < 4:
                            load_m(bl + 2)
                    S.op("dve", lambda e: e.tensor_tensor(out=mixTs[:, 6:8, :], in0=Onm[:], in1=sT[:, 17:19, :], op=ALU.mult), reads=["Onm", "sT"], writes=["mixTs"])
                    S.barrier()
                chk(6)

                with ExitStack() as p6:
                    ckb = sbt(p6, "ckb", [128, 16, 384], BF16)
                    cvb = sbt(p6, "cvb", [128, 16, 384], BF16)
                    ksT = sbt(p6, "ksT", [128, 3, 2056], BF16)
                    Cm = sbt(p6, "Cm", [128, 17, 8], BF16)
                    di_ = sbt(p6, "di_", [128, 136], I32)
                    da_ = sbt(p6, "da_", [128, 136], I32)
                    df_ = sbt(p6, "df_", [128, 136], F32)
                    m1_ = sbt(p6, "m1_", [128, 136], F32)
                    m2_ = sbt(p6, "m2_", [128, 136], F32)
                    m3_ = sbt(p6, "m3_", [128, 136], F32)
                    Ew = sbt(p6, "Ew", [128, 136], BF16)
                    zs = sbt(p6, "zs", [128, 8], F32)
                    S.op("pool", lambda e: e.iota(di_[:], pattern=[[128, 17], [-1, 8]], base=0, channel_multiplier=1), writes=["di_"])
                    S.op("dve", lambda e: e.tensor_copy(out=df_[:], in_=di_[:]), reads=["di_"], writes=["df_"])
                    S.op("dve", lambda e: e.tensor_scalar(out=m1_[:], in0=df_[:], scalar1=1920.0, scalar2=None, op0=ALU.is_ge), reads=["df_"], writes=["m1_"])
                    S.op("dve", lambda e: e.tensor_single_scalar(out=da_[:], in_=di_[:], scalar=3, op=ALU.bitwise_and), reads=["di_"], writes=["da_"])
                    S.op("dve", lambda e: e.tensor_scalar(out=m2_[:], in0=da_[:], scalar1=0.0, scalar2=None, op0=ALU.is_equal), reads=["da_"], writes=["m2_"])
                    S.op("dve", lambda e: e.tensor_scalar(out=m3_[:], in0=df_[:], scalar1=1536.0, scalar2=None, op0=ALU.is_ge), reads=["df_"], writes=["m3_"])
                    S.op("dve", lambda e: e.tensor_tensor(out=m2_[:], in0=m2_[:], in1=m3_[:], op=ALU.mult), reads=["m2_", "m3_"], writes=["m2_"])
                    S.op("dve", lambda e: e.tensor_tensor(out=m1_[:], in0=m1_[:], in1=m2_[:], op=ALU.add), reads=["m1_", "m2_"], writes=["m1_"])
                    S.op("dve", lambda e: e.tensor_single_scalar(out=da_[:], in_=di_[:], scalar=15, op=ALU.bitwise_and), reads=["di_", "m2_"], writes=["da_"])
                    S.op("dve", lambda e: e.tensor_scalar(out=m2_[:], in0=da_[:], scalar1=0.0, scalar2=None, op0=ALU.is_equal), reads=["da_"], writes=["m2_"])
                    S.op("dve", lambda e: e.tensor_tensor(out=m1_[:], in0=m1_[:], in1=m2_[:], op=ALU.add), reads=["m1_", "m2_"], writes=["m1_"])
                    S.op("dve", lambda e: e.tensor_scalar(out=m3_[:], in0=df_[:], scalar1=2048.0, scalar2=None, op0=ALU.is_le), reads=["df_", "m2_"], writes=["m3_"])
                    S.op("dve", lambda e: e.tensor_tensor(out=Cm[:].rearrange("p k q -> p (k q)"), in0=m1_[:], in1=m3_[:], op=ALU.mult), reads=["m1_", "m3_"], writes=["Cm"])
                    Cmf = Cm[:].rearrange("p k q -> p (k q)")
                    Cm3 = sbt(p6, "Cm3", [128, 17, 3, 8], BF16)
                    for h3 in range(3):
                        S.op("dve", lambda e, h3=h3: e.tensor_copy(out=Cm3[:, :, h3, :], in_=Cm[:]), reads=["Cm", "Cm3"], writes=["Cm3"])
                    Cm3f = Cm3[:].rearrange("p k h q -> p (k h q)")
                    roles.update({"pT": [0, 1, 2, 3], "pS": [4, 5], "pO": [6, 7]})
                    for bk in roles["pS"]:
                        S.op("dve", lambda e, bk=bk: e.memset(PA(bk)[:], 0.0), writes=["pB%d" % bk])
                    ckb2 = [ckb, sbt(p6, "ckb_1", [128, 16, 384], BF16)]
                    cvb2 = [cvb, sbt(p6, "cvb_1", [128, 16, 384], BF16)]
                    ksT2 = [ksT, sbt(p6, "ksT_1", [128, 3, 2056], BF16)]
                    Ew2 = [sbt(p6, "Ew_%d" % i, [128, 408], BF16) for i in range(2)]
                    zs2 = [sbt(p6, "zs_%d" % i, [128, 3, 8], F32) for i in range(2)]
                    On = sbt(p6, "On", [128, 3, 32], F32)

                    def load_bl(bl):
                        sl_ = bl % 2
                        S.dma("pool", ckb2[sl_][:], ck[bl].rearrange("(t p) n -> p t n", p=128), writes=["ckb%d" % sl_])
                        S.dma("pool", cvb2[sl_][:], cv[bl].rearrange("(t p) n -> p t n", p=128), writes=["cvb%d" % sl_])

                    load_bl(0)
                    load_bl(1)
                    for bl in range(4):
                        sl_ = bl % 2
                        for ft in range(3):
                            for b0 in range(0, 16, 8):
                                ps_ = nxt("pT")
                                for k_ in range(8):
                                    S.op("pe", lambda e, k_=k_, b0=b0, ft=ft: e.transpose(PT(ps_)[:, 128 * k_:128 * k_ + 128], ckb2[sl_][:, b0 + k_, 128 * ft:128 * ft + 128], ident[:]),
                                         reads=["ckb%d" % sl_, "ident"], writes=["pB%d" % ps_])
                                cp(ksT2[sl_][:, ft, 128 * b0:128 * b0 + 1024], PT(ps_)[:, :], ["pB%d" % ps_], ["ksT%d" % sl_])
                        S.op("dve", lambda e, bl=bl: e.tensor_copy(out=ksT2[sl_][:, :, 2048:2056], in_=sT[:, 3:6, 8 * bl:8 * bl + 8]), reads=["sT", "ksT%d" % sl_], writes=["ksT%d" % sl_])
                        hst = {}
                        for half in range(2):
                            s_ = nxt("pS")
                            hst[half] = s_
                            for h3 in range(3):
                                h = 3 * half + h3
                                ft, hh = h // 2, h % 2
                                rows = slice(64 * hh, 64 * hh + 64)
                                qs = qsz[:, h, 8 * bl:8 * bl + 8]
                                c0 = 136 * h3
                                for kt in range(16):
                                    S.op("pe", lambda e, kt=kt, ft=ft, rows=rows, qs=qs, c0=c0, h3=h3: e.matmul(PA(s_)[:, 24 * kt + 8 * h3:24 * kt + 8 * h3 + 8], ksT2[sl_][:, ft, 128 * kt:128 * kt + 128], qs,
                                                                                                      start=True, stop=True),
                                         reads=["ksT%d" % sl_, "qsz"], writes=["pB%d" % s_])
                                S.op("pe", lambda e, ft=ft, rows=rows, qs=qs, c0=c0, h3=h3: e.matmul(PA(s_)[0:8, 384 + 8 * h3:384 + 8 * h3 + 8], ksT2[sl_][:, ft, 2048:2056], qs, start=True, stop=True),
                                     reads=["ksT%d" % sl_, "qsz"], writes=["pB%d" % s_])
                        for half in range(2):
                            s_ = hst[half]
                            S.op("act", lambda e, half=half, s_=s_: e.activation(out=Ew2[half][:], in_=PA(s_)[:, 0:408], func=AF.Exp, scale=0.125), reads=["pB%d" % s_], writes=["Ew_%d" % half])
                            S.op("dve", lambda e, half=half: e.tensor_tensor(out=Ew2[half][:], in0=Ew2[half][:], in1=Cm3f, op=ALU.mult), reads=["Ew_%d" % half, "Cm3"], writes=["Ew_%d" % half])
                        ost = {}
                        for half in range(2):
                            oslot = nxt("pO")
                            ost[half] = oslot
                            for h3 in range(3):
                                h = 3 * half + h3
                                hh = h % 2
                                rows = slice(64 * hh, 64 * hh + 64)
                                c0 = 136 * h3
                                for kt in range(16):
                                    S.op("pe", lambda e, kt=kt, h=h, h3=h3, rows=rows, c0=c0, half=half: e.matmul(PA(oslot)[rows, 8 * h3:8 * h3 + 8], cvb2[sl_][:, kt, 64 * h:64 * h + 64],
                                                                                                               Ew2[half][:, 24 * kt + 8 * h3:24 * kt + 8 * h3 + 8],
                                                                                                               start=(kt == 0), stop=False, skip_group_check=True),
                                         reads=["cvb%d" % sl_, "Ew_%d" % half], writes=["pB%d" % oslot])
                                S.op("pe", lambda e, h=h, h3=h3, rows=rows, c0=c0, half=half: e.matmul(PA(oslot)[rows, 8 * h3:8 * h3 + 8], vnew[0:8, bl, 64 * h:64 * h + 64],
                                                                                                    Ew2[half][0:8, 384 + 8 * h3:384 + 8 * h3 + 8], start=False, stop=True, skip_group_check=True),
                                     reads=["vnew", "Ew_%d" % half], writes=["pB%d" % oslot])
                            for hf in range(2):
                                hr = slice(64 * hf, 64 * hf + 64)
                                for kt in range(16):
                                    S.op("pe", lambda e, half=half, kt=kt, hr=hr: e.matmul(PA(oslot)[hr, 32:56], onesb[:, 0:64], Ew2[half][:, 24 * kt:24 * kt + 24], start=(kt == 0), stop=False, skip_group_check=True),
                                         reads=["onesb", "Ew_%d" % half], writes=["pB%d" % oslot])
                                S.op("pe", lambda e, half=half, hr=hr: e.matmul(PA(oslot)[hr, 32:56], onesb[0:8, 0:64], Ew2[half][0:8, 384:408], start=False, stop=True, skip_group_check=True),
                                     reads=["onesb", "Ew_%d" % half], writes=["pB%d" % oslot])
                        if bl + 2 < 4:
                            load_bl(bl + 2)
                        for half in range(2):
                            oslot = ost[half]
                            for hf in range(2):
                                hr = slice(64 * hf, 64 * hf + 64)
                                S.op("dve", lambda e, half=half, oslot=oslot, hr=hr: e.reciprocal(out=zs2[half][hr].rearrange("p h q -> p (h q)"), in_=PA(oslot)[hr, 32:56]),
                                     reads=["pB%d" % oslot, "zs_%d" % half], writes=["zs_%d" % half])
                            for h3 in range(3):
                                h = 3 * half + h3
                                ft, hh = h // 2, h % 2
                                rows = slice(64 * hh, 64 * hh + 64)
                                S.op("dve", lambda e, half=half, oslot=oslot, h3=h3, ft=ft, rows=rows: e.tensor_tensor(out=On[rows, ft, 8 * bl:8 * bl + 8], in0=PA(oslot)[rows, 8 * h3:8 * h3 + 8],
                                                                                                               in1=zs2[half][rows, h3, :], op=ALU.mult),
                                     reads=["pB%d" % oslot, "zs_%d" % half, "On"], writes=["On"])
                    S.op("dve", lambda e: e.tensor_tensor(out=mixTs[:, 0:3, :], in0=On[:], in1=sT[:, 11:14, :], op=ALU.mult), reads=["On", "sT"], writes=["mixTs"])
                    S.barrier()
                chk(7)

                with ExitStack() as p5:
                    wo = sbt(p5, "wo", [128, 8, 1024], BF16)
                    gpost = sbt(p5, "gpost", [128, 1024], F32)
                    xr = [sbt(p5, "xr%d" % i, [128, 1024], F32) for i in range(2)]
                    yo = [sbt(p5, "yo%d" % i, [128, 1024], F32) for i in range(2)]
                    junk2 = sbt(p5, "junk2", [128, 512], BF16)
                    ss2 = sbt(p5, "ss2", [128, 64], F32)
                    st2 = sbt(p5, "st2", [128, 32], F32)
                    rs2 = sbt(p5, "rs2", [128, 32], F32)
                    S.dma("pool", wo[:], w_out.rearrange("(c p) n -> p c n", p=128), writes=["wo"])
                    S.dma("sp", gpost[:], g_post[0:1, :].to_broadcast([128, 1024]), writes=["gpost"])
                    S.op("pool", lambda e: e.memset(ss2[:], 0.0), writes=["ss2"])
                    roles.update({"pA": [0, 1, 2, 3, 4, 5, 6, 7]})
                    bulk_issue(len(bulk_jobs))

                    def out_tile(idx, lhs_fn, npart, x_src, y_dst):
                        sl_ = idx % 2
                        S.dma("sp", xr[sl_][0:npart, :], x_src, writes=["xr%d" % sl_])
                        pas = []
                        for half in range(2):
                            pa = nxt("pA")
                            pas.append(pa)
                            for c in range(8):
                                S.op("pe", lambda e, c=c, half=half, pa=pa: e.matmul(PA(pa)[0:npart, :], lhs_fn(c), wo[:, c, 512 * half:512 * half + 512], start=(c == 0), stop=(c == 7)),
                                     reads=["wo", "mixall"], writes=["pB%d" % pa])
                            S.op("act", lambda e, half=half, pa=pa: e.activation(out=junk2[0:npart, :], in_=PA(pa)[0:npart, :], func=AF.Square,
                                                                                accum_out=ss2[0:npart, 2 * idx + half:2 * idx + half + 1]),
                                 reads=["pB%d" % pa, "ss2"], writes=["junk2", "ss2_%d_%d" % (idx, half)])
                            hs_ = slice(512 * half, 512 * half + 512)
                            S.op("dve", lambda e, half=half, pa=pa, hs_=hs_: e.tensor_tensor(out=yo[sl_][0:npart, hs_], in0=PA(pa)[0:npart, :], in1=gpost[0:npart, hs_], op=ALU.mult),
                                 reads=["pB%d" % pa, "gpost", "yo%d" % sl_], writes=["yo%d" % sl_])
                        S.op("pool", lambda e: e.tensor_tensor(out=st2[0:npart, idx:idx + 1], in0=ss2[0:npart, 2 * idx:2 * idx + 1], in1=ss2[0:npart, 2 * idx + 1:2 * idx + 2], op=ALU.add),
                             reads=["ss2_%d_0" % idx, "ss2_%d_1" % idx], writes=["st2_%d" % idx])
                        S.op("pool", lambda e: e.tensor_scalar(out=st2[0:npart, idx:idx + 1], in0=st2[0:npart, idx:idx + 1], scalar1=1.0 / 1024, scalar2=EPS, op0=ALU.mult, op1=ALU.add),
                             reads=["st2_%d" % idx], writes=["st2_%d" % idx])
                        S.op("pool", lambda e: e.tensor_tensor(out=rs2[0:npart, idx:idx + 1], in0=st2[0:npart, idx:idx + 1], in1=mhalf[0:npart, 0:1], op=ALU.pow),
                             reads=["st2_%d" % idx, "mhalf"], writes=["rs2_%d" % idx])
                        for half in range(2):
                            hs = slice(512 * half, 512 * half + 512)
                            S.op("dve", lambda e, half=half, hs=hs: e.scalar_tensor_tensor(out=yo[sl_][0:npart, hs], in0=yo[sl_][0:npart, hs], scalar=rs2[0:npart, idx:idx + 1],
                                                                                          in1=xr[sl_][0:npart, hs], op0=ALU.mult, op1=ALU.add),
                                 reads=["rs2_%d" % idx, "xr%d" % sl_, "yo%d" % sl_], writes=["yo%d" % sl_])
                        S.dma("sp", y_dst, yo[sl_][0:npart, :], reads=["yo%d" % sl_])

                    for tt in range(16):
                        out_tile(tt, lambda c, tt=tt: mixT[:, c, 128 * tt:128 * tt + 128], 128, xe[2048 + 128 * tt:2048 + 128 * tt + 128, :], yp[128 * tt:128 * tt + 128, :])
                    out_tile(16, lambda c: mixTs[:, c, :], 32, xs[:, :], ys[:, :])
                    S.barrier()
        except _Stop:
            pass
        import os
        S.dead = False
        if os.environ.get("KDEBUG"):
            dbg = nc.dram_tensor("dbg", [128, 8, 2048], F32, kind="ExternalOutput").ap()
            S.dma("pool", dbg[:], mixT[:], reads=["mix%d" % i for i in range(8)])
            dbgs = nc.dram_tensor("dbgs", [128, 8, 32], F32, kind="ExternalOutput").ap()
            S.dma("pool", dbgs[:], mixTs[:], reads=["mixTs"])
        S.finish()
    return nc


def _layout_inputs(inp):
    f32 = np.float32
    xp = np.asarray(inp["x_prompt"], f32)
    inv = (500000.0 ** (-np.arange(0, 16, 2, dtype=np.float32) / 16.0)).astype(f32)
    rcst = np.zeros((128, 2), f32)
    for p in range(128):
        i = p % 64
        if i < 16:
            rcst[p, 0] = inv[i % 8]
            rcst[p, 1] = -1.0 if i < 8 else 1.0
    rcs = np.zeros((32, 49), f32)
    rcs[:, 0] = 16384 + (np.arange(32) % 8)
    rcs[:, 1:] = np.tile(inv, 6)[None, :]
    vecs = np.stack([np.asarray(inp[k], f32).reshape(3, 128).T for k in ("b_dw", "ln_conv_g", "ln_conv_b", "b_pw2")], axis=1)
    w_dwT = np.ascontiguousarray(np.asarray(inp["w_dw"], f32)[0].T.reshape(3, 128, 31).transpose(1, 0, 2))
    shared = {
        "rcst": rcst, "rcs": rcs, "vecs": np.ascontiguousarray(vecs), "w_dwT": w_dwT,
        "w_in": np.asarray(inp["w_in"], f32)[0], "w_out": np.asarray(inp["w_out"], f32)[0],
        "w_mkv": np.asarray(inp["w_mem_kv"], f32)[0], "w_pw2": np.asarray(inp["w_pw2"], f32)[0],
        "g_pre": np.asarray(inp["norm_pre"], f32).reshape(1, 1024), "g_post": np.asarray(inp["norm_post"], f32).reshape(1, 1024),
        "g_mem": np.asarray(inp["norm_mem"], f32).reshape(1, 1024),
    }
    maps = []
    for core in range(8):
        b, c = core // 4, core % 4
        t0 = 2048 * c
        xe = np.zeros((4096, 1024), f32)
        if c > 0:
            xe[:2048] = xp[b, t0 - 2048:t0]
        xe[2048:] = xp[b, t0:t0 + 2048]
        posv = np.maximum(t0 - 2048 + np.arange(4096), 0).astype(f32).reshape(1, 4096)
        m = dict(shared)
        m.update({
            "xe": xe, "pos": posv, "hflag": np.full((128, 1), 1.0 if c > 0 else 0.0, f32),
            "xs": np.ascontiguousarray(np.asarray(inp["x_sample"], f32)[4 * core:4 * core + 4].reshape(32, 1024)),
            "ck": np.ascontiguousarray(np.asarray(inp["cache_win_k"], f32)[0, 4 * core:4 * core + 4].reshape(4, 2048, 384)),
            "cv": np.ascontiguousarray(np.asarray(inp["cache_win_v"], f32)[0, 4 * core:4 * core + 4].reshape(4, 2048, 384)),
            "sc": np.ascontiguousarray(np.asarray(inp["state_conv"], f32)[0, 4 * core:4 * core + 4]),
            "cmk": np.ascontiguousarray(np.asarray(inp["cache_mem_k"], f32)[0, 4 * core:4 * core + 4].reshape(4, 256, 256)),
            "cmv": np.ascontiguousarray(np.asarray(inp["cache_mem_v"], f32)[0, 4 * core:4 * core + 4].reshape(4, 256, 256)),
            "mem": np.ascontiguousarray(np.asarray(inp["mem_prompt"], f32)[b]),
        })
        maps.append(m)
    return maps


def kernel(**inp):
    maps = _layout_inputs(inp)
    nc = build_nc()
    res = run_bass_kernel_spmd(nc, maps, core_ids=list(range(8)))
    R = res.results
    f32 = np.float32
    y_p = np.zeros((2, 8192, 1024), f32)
    for core in range(8):
        b, c = core // 4, core % 4
        y_p[b, 2048 * c:2048 * c + 2048] = R[core]["yp"]
    y_s = np.concatenate([R[core]["ys"].reshape(4, 8, 1024) for core in range(8)], axis=0)
    kpo = np.stack([R[4 * b + 3]["kp"].reshape(2048, 6, 64) for b in range(2)])[None]
    vpo = np.stack([R[4 * b + 3]["vp"].reshape(2048, 6, 64) for b in range(2)])[None]
    cvo = np.stack([R[4 * b + 3]["cvp"] for b in range(2)])[None]
    mko = np.stack([R[4 * b]["mkp"].reshape(256, 4, 64) for b in range(2)])[None]
    mvo = np.stack([R[4 * b]["mvp"].reshape(256, 4, 64) for b in range(2)])[None]
    kso = np.concatenate([R[core]["ksn"].reshape(4, 2048, 6, 64) for core in range(8)], axis=0)[None]
    vso = np.concatenate([R[core]["vsn"].reshape(4, 2048, 6, 64) for core in range(8)], axis=0)[None]
    cso = np.concatenate([R[core]["csn"] for core in range(8)], axis=0)[None]
    return (y_p, y_s, kpo, vpo, cvo, mko, mvo, kso, vso, cso)
```

```python
import math
from contextlib import ExitStack
import numpy as np
import concourse.bass as bass
import concourse.mybir as mybir
from concourse.bass_utils import run_bass_kernel_spmd

F32 = mybir.dt.float32
BF16 = mybir.dt.bfloat16
I32 = mybir.dt.int32
AF = mybir.ActivationFunctionType
ALU = mybir.AluOpType
AX = mybir.AxisListType
EPS = 1e-6
TWO_PI = 2.0 * math.pi
C1 = 6.28125
C2 = TWO_PI - C1
PI_CL = 3.1415925


class Sched:
    def __init__(self, nc, stack, n_dma_sems=10):
        self.nc = nc
        self.h = {"pe": nc.tensor, "act": nc.scalar, "dve": nc.vector, "pool": nc.gpsimd, "sp": nc.sync}
        self.sem = {k: stack.enter_context(nc.semaphore("s_" + k)) for k in self.h}
        self.cnt = {k: 0 for k in self.h}
        self.waited = {k: {} for k in self.h}
        self.lastw = {}
        self.readers = {}
        self.dma_sems = {}
        for q in ("sp", "act", "pool"):
            self.dma_sems[q] = [[stack.enter_context(nc.semaphore("d_%s_%d" % (q, i))), 0] for i in range(n_dma_sems)]
        self.dma_sems["bulk"] = [[stack.enter_context(nc.semaphore("d_bulk_%d" % i)), 0] for i in range(24)]
        self.dma_rr = {q: 0 for q in self.dma_sems}
        self.n_inst = 0
        self.dead = False

    def _wait(self, eng, ev):
        sem, val, src = ev
        if src == eng and eng == "pe":
            return
        key = id(sem)
        if self.waited[eng].get(key, 0) >= val:
            return
        self.h[eng].wait_ge(sem, val)
        self.waited[eng][key] = val

    def _deps(self, eng, reads, writes):
        for b in reads:
            ev = self.lastw.get(b)
            if ev is not None:
                self._wait(eng, ev)
        for b in writes:
            ev = self.lastw.get(b)
            if ev is not None:
                self._wait(eng, ev)
            for ev in self.readers.get(b, ()):
                self._wait(eng, ev)

    def _record(self, ev, reads, writes):
        for b in reads:
            self.readers.setdefault(b, []).append(ev)
        for b in writes:
            self.lastw[b] = ev
            self.readers[b] = []

    def op(self, eng, fn, reads=(), writes=()):
        if self.dead:
            return None
        pr = [b for b in reads if b[:2] == "pB"]
        if pr:
            reads = [b for b in reads if b not in pr]
            writes = list(writes) + pr
        self._deps(eng, reads, writes)
        inst = fn(self.h[eng])
        self.cnt[eng] += 1
        inst.then_inc(self.sem[eng], 1)
        ev = (self.sem[eng], self.cnt[eng], eng)
        self._record(ev, reads, writes)
        self.n_inst += 1
        return ev

    def dma(self, q, out, in_, reads=(), writes=(), sems=None, **kw):
        if self.dead:
            return None
        self._deps(q, reads, writes)
        sq = sems or q
        slot = self.dma_sems[sq][self.dma_rr[sq] % len(self.dma_sems[sq])]
        self.dma_rr[sq] += 1
        sem, val = slot
        if val > 0:
            self._wait(q, (sem, val, "dma"))
        inst = self.h[q].dma_start(out=out, in_=in_, **kw)
        slot[1] = val + 16
        inst.then_inc(sem, 16)
        ev = (sem, val + 16, "dma")
        self._record(ev, reads, writes)
        self.n_inst += 1
        return ev

    def barrier(self):
        if self.dead:
            return
        for e in self.h:
            for k in self.h:
                if k != e and self.cnt[k] > 0:
                    self._wait(e, (self.sem[k], self.cnt[k], k))
            for q in self.dma_sems:
                for sem, val in self.dma_sems[q]:
                    if val > 0:
                        self._wait(e, (sem, val, "dma"))
        self.lastw = {}
        self.readers = {}

    def finish(self, eng="sp"):
        for q in self.dma_sems:
            for sem, val in self.dma_sems[q]:
                if val > 0:
                    self.waited[eng].pop(id(sem), None)
                    self.h[eng].wait_ge(sem, val)
        for k in self.h:
            if k != eng and self.cnt[k] > 0:
                self.h[eng].wait_ge(self.sem[k], self.cnt[k])


class _Stop(Exception):
    pass


def sl(start, n, step):
    return slice(start, start + step * (n - 1) + 1, step)


def build_nc():
    nc = bass.Bass("TRN2", target_bir_lowering=False)

    def DI(name, shape):
        return nc.dram_tensor(name, shape, F32, kind="ExternalInput").ap()

    def DO(name, shape):
        return nc.dram_tensor(name, shape, F32, kind="ExternalOutput").ap()

    xe = DI("xe", [4096, 1024]); pos = DI("pos", [1, 4096]); hflag = DI("hflag", [128, 1])
    rcst = DI("rcst", [128, 2]); xs = DI("xs", [32, 1024]); rcs = DI("rcs", [32, 49])
    ck = DI("ck", [4, 2048, 384]); cv = DI("cv", [4, 2048, 384]); sc = DI("sc", [4, 30, 384])
    cmk = DI("cmk", [4, 256, 256]); cmv = DI("cmv", [4, 256, 256]); mem = DI("mem", [256, 1024])
    w_in = DI("w_in", [1024, 3200]); w_out = DI("w_out", [1024, 1024]); w_mkv = DI("w_mkv", [1024, 512])
    w_dwT = DI("w_dwT", [128, 3, 31]); w_pw2 = DI("w_pw2", [384, 384])
    g_pre = DI("g_pre", [1, 1024]); g_post = DI("g_post", [1, 1024]); g_mem = DI("g_mem", [1, 1024])
    vecs = DI("vecs", [128, 4, 3])
    yp = DO("yp", [2048, 1024]); ys = DO("ys", [32, 1024])
    kp = DO("kp", [2048, 384]); vp = DO("vp", [2048, 384]); cvp = DO("cvp", [30, 384])
    mkp = DO("mkp", [256, 256]); mvp = DO("mvp", [256, 256])
    ksn = DO("ksn", [4, 2048, 384]); vsn = DO("vsn", [4, 2048, 384]); csn = DO("csn", [4, 30, 384])

    w_in_v = w_in.rearrange("(c p) n -> p c n", p=128)

    import os
    STOP = float(os.environ.get("KSTOP", "99"))
    with ExitStack() as st:
        S = Sched(nc, st)

        def chk(level):
            if STOP <= level:
                S.barrier()
                S.dead = True

        def sbt(stack, name, shape, dt):
            return stack.enter_context(nc.sbuf_tensor(name, shape, dt))

        def pst(name, shape, dt):
            return st.enter_context(nc.psum_tensor(name, shape, dt))

        PB = [pst("pB%d" % i, [128, 512], F32) for i in range(8)]
        roles = {"pT": [0, 1], "pA": [2, 3], "pS": [4, 5], "pO": [6, 7]}
        rr = {"pT": 0, "pA": 0, "pS": 0, "pO": 0, "cp": 0, "w": 0, "kb": 0, "eb": 0, "rzs": 0}

        def nxt(k, n=2):
            if k in roles:
                v = roles[k][rr[k] % len(roles[k])]
            else:
                v = rr[k] % n
            rr[k] += 1
            return v

        def PA(b):
            return PB[b]

        def PT(b):
            return PB[b][:].bitcast(BF16)

        def cp(out, in_, reads, writes):
            if nxt("cp") == 0:
                return S.op("act", lambda e: e.copy(out, in_), reads=reads, writes=writes)
            return S.op("dve", lambda e: e.tensor_copy(out=out, in_=in_), reads=reads, writes=writes)

        ident = sbt(st, "ident", [128, 128], BF16)
        onesb = sbt(st, "onesb", [128, 128], BF16)
        mask4 = sbt(st, "mask4", [128, 512], BF16)
        Rm = sbt(st, "Rm", [128, 128], BF16)
        mhalf = sbt(st, "mhalf", [128, 8], F32)
        epst = sbt(st, "epst", [128, 1], F32)
        rc_sb = sbt(st, "rc_sb", [128, 2], F32)
        vec_sb = sbt(st, "vec_sb", [128, 4, 3], F32)
        hfl = sbt(st, "hfl", [128, 1], F32)
        ss = sbt(st, "ss", [128, 64], F32)
        ms = sbt(st, "ms", [128, 64], F32)
        rstd = sbt(st, "rstd", [128, 64], F32)
        hTh32 = sbt(st, "hTh32", [128, 8, 32], BF16)
        hTo = sbt(st, "hTo", [128, 8, 2048], BF16)
        hTs = sbt(st, "hTs", [128, 8, 32], BF16)
        hTm = sbt(st, "hTm", [128, 8, 256], BF16)
        mixT = sbt(st, "mixT", [128, 8, 2048], BF16)
        mixTs = sbt(st, "mixTs", [128, 8, 32], BF16)
        wst = [sbt(st, "wst%d" % i, [128, 8, 128], BF16) for i in range(4)]
        ts32 = sbt(st, "ts32", [32, 3200], F32)

        S.op("pool", lambda e: e.memset(onesb[:], 1.0), writes=["onesb"])
        S.op("pool", lambda e: e.memset(mhalf[:], -0.5), writes=["mhalf"])
        S.op("pool", lambda e: e.memset(epst[:], EPS), writes=["epst"])
        S.op("pool", lambda e: e.memset(ss[:], 0.0), writes=["ss"])
        S.op("pool", lambda e: e.affine_select(out=ident[:], in_=onesb[:], pattern=[[1, 128]], compare_op=ALU.is_equal,
                                               fill=0.0, base=0, channel_multiplier=-1), reads=["onesb"], writes=["ident"])
        for blk in range(4):
            if blk % 2 == 0:
                S.op("pool", lambda e, blk=blk: e.affine_select(out=mask4[:, 128 * blk:128 * blk + 128], in_=onesb[:], pattern=[[-1, 128]],
                                                                compare_op=ALU.is_ge, fill=0.0, base=0, channel_multiplier=1),
                     reads=["onesb"], writes=["mask4"])
            else:
                S.op("pool", lambda e, blk=blk: e.affine_select(out=mask4[:, 128 * blk:128 * blk + 128], in_=onesb[:], pattern=[[1, 128]],
                                                                compare_op=ALU.is_ge, fill=0.0, base=0, channel_multiplier=-1),
                     reads=["onesb"], writes=["mask4"])
        S.op("pool", lambda e: e.memset(Rm[:], 0.0), writes=["Rm"])
        for hb in range(2):
            c0 = 64 * hb
            S.op("pool", lambda e, c0=c0: e.affine_select(out=Rm[:, c0:c0 + 8], in_=onesb[:, 0:8], pattern=[[-1, 8]], compare_op=ALU.is_equal,
                                                          fill=0.0, base=-c0 - 8, channel_multiplier=1), reads=["onesb"], writes=["Rm"])
            S.op("pool", lambda e, c0=c0: e.affine_select(out=Rm[:, c0 + 8:c0 + 16], in_=onesb[:, 0:8], pattern=[[-1, 8]], compare_op=ALU.is_equal,
                                                          fill=0.0, base=-c0, channel_multiplier=1), reads=["onesb"], writes=["Rm"])
        S.dma("sp", rc_sb[:], rcst[:], writes=["rc_sb"])
        S.dma("sp", vec_sb[:], vecs[:], writes=["vec_sb"])
        S.dma("sp", hfl[:], hflag[:], writes=["hfl"])

        bulk_jobs = []
        for bl_ in range(4):
            for q4 in range(4):
                r0 = 510 * q4
                bulk_jobs.append((ksn[bl_, r0:r0 + 510, :], ck[bl_, 8 + r0:8 + r0 + 510, :]))
                bulk_jobs.append((vsn[bl_, r0:r0 + 510, :], cv[bl_, 8 + r0:8 + r0 + 510, :]))
            bulk_jobs.append((csn[bl_, 0:22, :], sc[bl_, 8:30, :]))

        def bulk_issue(n):
            for _ in range(n):
                if bulk_jobs:
                    o_, i_ = bulk_jobs.pop(0)
                    S.dma("sp", o_, i_, sems="bulk")

        NW = len(wst)
        wq_cols = []
        for hp_ in range(3):
            wq_cols += [384 + 128 * hp_, 128 * hp_, 768 + 128 * hp_, 1152 + 128 * hp_]
        for cc_ in range(3):
            wq_cols += [1536 + 128 * cc_, 1920 + 128 * cc_, 2304 + 128 * cc_]
        for mp_ in range(2):
            wq_cols += [2688 + 128 * mp_, 2944 + 128 * mp_]
        wstate = {"issued": 0, "used": 0}

        def w_issue(upto):
            while wstate["issued"] < min(upto, len(wq_cols)):
                i = wstate["issued"]
                s_ = i % NW
                S.dma("pool", wst[s_][:], w_in_v[:, :, wq_cols[i]:wq_cols[i] + 128], writes=["wst%d" % s_])
                wstate["issued"] += 1

        def wtile(col0):
            i = wstate["used"]
            assert wq_cols[i] == col0, (i, col0)
            wstate["used"] += 1
            if i >= 1:
                bulk_issue(2)
            w_issue(i + 1)
            w_issue(i + NW - 1)
            return i % NW

        def hT_rhs(c, g):
            if g < 4:
                return hTh[:, c, 512 * g:512 * g + 512]
            return hTo[:, c, 512 * (g - 4):512 * (g - 4) + 512]

        def proj(ws, g, pa, ntok=512, rhs_fn=None):
            for c in range(8):
                rhs = rhs_fn(c) if rhs_fn is not None else hT_rhs(c, g)
                S.op("pe", lambda e, c=c, rhs=rhs: e.matmul(PA(pa)[:, 0:ntok], wst[ws][:, c, :], rhs, start=(c == 0), stop=(c == 7)),
                     reads=["wst%d" % ws, "hT"], writes=["pB%d" % pa])

        def proj_s(ws, col0):
            pa = nxt("pA")
            for c in range(8):
                S.op("pe", lambda e, c=c: e.matmul(PA(pa)[0:32, 0:128], hTs[:, c, :], wst[ws][:, c, :], start=(c == 0), stop=(c == 7)),
                     reads=["wst%d" % ws, "hT"], writes=["pB%d" % pa])
            cp(ts32[:, col0:col0 + 128], PA(pa)[0:32, 0:128], ["pB%d" % pa], ["ts32"])

        try:
            w_issue(NW - 1)
            with ExitStack() as ph:
                hTh = sbt(ph, "hTh", [128, 8, 2048], BF16)
                cosT = sbt(ph, "cosT", [128, 4096], BF16)
                sinT = sbt(ph, "sinT", [128, 4096], BF16)
                with ExitStack() as p1:
                    xt = [sbt(p1, "xt%d" % i, [128, 1024], F32) for i in range(6)]
                    xn = [sbt(p1, "xn%d" % i, [128, 1024], BF16) for i in range(4)]
                    roles.update({"pT": [0, 1, 2, 3]})
                    junk3 = [sbt(p1, "jnk%d" % i, [128, 1024], BF16) for i in range(4)]
                    gpre = sbt(p1, "gpre", [128, 1024], F32)
                    gmem = sbt(p1, "gmem", [128, 1024], F32)
                    S.dma("sp", gpre[:], g_pre[0:1, :].to_broadcast([128, 1024]), writes=["gpre"])
                    S.dma("sp", gmem[:], g_mem[0:1, :].to_broadcast([128, 1024]), writes=["gmem"])
                    posb = sbt(p1, "posb", [128, 512], F32)
                    ta = sbt(p1, "ta", [128, 512], F32)
                    tb = sbt(p1, "tb", [128, 512], F32)
                    tk = sbt(p1, "tk", [128, 512], I32)
                    tf = sbt(p1, "tf", [128, 512], F32)
                    tk2 = sbt(p1, "tk2", [128, 512], I32)
                    tf2 = sbt(p1, "tf2", [128, 512], F32)

                    def table_chunk(ch):
                        cs = slice(512 * ch, 512 * ch + 512)
                        S.dma("sp", posb[:], pos[0:1, cs].to_broadcast([128, 512]), writes=["posb"])
                        S.op("dve", lambda e: e.tensor_scalar(out=ta[:], in0=posb[:], scalar1=rc_sb[:, 0:1], scalar2=None, op0=ALU.mult),
                             reads=["posb", "rc_sb"], writes=["ta"])
                        S.op("dve", lambda e: e.tensor_scalar(out=tb[:], in0=ta[:], scalar1=math.pi / 2, scalar2=None, op0=ALU.add), reads=["ta"], writes=["tb"])
                        ch_ = [(ta, "ta", tk, "tk", tf, "tf"), (tb, "tb", tk2, "tk2", tf2, "tf2")]
                        for (src, sname, k_, kn, f_, fn) in ch_:
                            S.op("dve", lambda e, src=src, k_=k_: e.tensor_scalar(out=k_[:], in0=src[:], scalar1=1.0 / TWO_PI, scalar2=None, op0=ALU.mult), reads=[sname], writes=[kn])
                        for (src, sname, k_, kn, f_, fn) in ch_:
                            S.op("dve", lambda e, k_=k_, f_=f_: e.tensor_copy(out=f_[:], in_=k_[:]), reads=[kn], writes=[fn])
                        for cst in (-C1, -C2):
                            for (src, sname, k_, kn, f_, fn) in ch_:
                                S.op("dve", lambda e, src=src, f_=f_, cst=cst: e.scalar_tensor_tensor(out=src[:], in0=f_[:], scalar=cst, in1=src[:], op0=ALU.mult, op1=ALU.add),
                                     reads=[fn, sname], writes=[sname])
                        for (src, sname, k_, kn, f_, fn) in ch_:
                            S.op("dve", lambda e, src=src: e.tensor_scalar(out=src[:], in0=src[:], scalar1=-PI_CL, scalar2=PI_CL, op0=ALU.max, op1=ALU.min), reads=[sname], writes=[sname])
                        S.op("act", lambda e: e.activation(out=sinT[:, cs], in_=ta[:], func=AF.Sin, scale=rc_sb[:, 1:2]), reads=["ta", "rc_sb"], writes=["sinT"])
                        S.op("act", lambda e: e.activation(out=cosT[:, cs], in_=tb[:], func=AF.Sin), reads=["tb"], writes=["cosT"])
                    cnt = [0]

                    def norm_a(job):
                        src, npart, gt, gname, dst = job
                        i = cnt[0]
                        cnt[0] += 1
                        xs_, ns_ = i % 6, i % 4
                        S.dma("sp", xt[xs_][0:npart, :], src, writes=["xt%d" % xs_])
                        S.op("act", lambda e: e.activation(out=junk3[ns_][0:npart, :], in_=xt[xs_][0:npart, :], func=AF.Square, accum_out=ss[0:npart, i:i + 1]),
                             reads=["xt%d" % xs_, "ss"], writes=["junk%d" % ns_, "ss%d" % i])
                        S.op("pool", lambda e: e.tensor_scalar(out=ms[0:npart, i:i + 1], in0=ss[0:npart, i:i + 1], scalar1=1.0 / 1024, scalar2=EPS,
                                                               op0=ALU.mult, op1=ALU.add), reads=["ss%d" % i], writes=["ms%d" % i])
                        S.op("pool", lambda e: e.tensor_tensor(out=rstd[0:npart, i:i + 1], in0=ms[0:npart, i:i + 1], in1=mhalf[0:npart, 0:1], op=ALU.pow),
                             reads=["ms%d" % i, "mhalf"], writes=["rstd%d" % i])
                        S.op("dve", lambda e: e.scalar_tensor_tensor(out=xn[ns_][0:npart, :], in0=xt[xs_][0:npart, :], scalar=rstd[0:npart, i:i + 1],
                                                                     in1=gt[0:npart, :], op0=ALU.mult, op1=ALU.mult),
                             reads=["xt%d" % xs_, "rstd%d" % i, gname], writes=["xn%d" % ns_])
                        return ns_

                    def norm_b(job, ns_):
                        src, npart, gt, gname, dst = job
                        ps_ = nxt("pT")
                        for c in range(8):
                            S.op("pe", lambda e, c=c: e.transpose(PT(ps_)[:, c * 128:c * 128 + npart], xn[ns_][0:npart, c * 128:(c + 1) * 128],
                                                                  ident[0:npart, 0:npart]),
                                 reads=["xn%d" % ns_, "ident"], writes=["pB%d" % ps_])
                        src_ = PT(ps_)[:, :].rearrange("p (c t) -> p c t", t=128)[:, :, 0:npart]
                        S.op("act", lambda e: e.copy(dst, src_), reads=["pB%d" % ps_], writes=["hT"])

                    jobs = []
                    for tt in range(32):
                        if tt < 16:
                            dst = hTh[:, :, 128 * tt:128 * tt + 128]
                        else:
                            dst = hTo[:, :, 128 * (tt - 16):128 * (tt - 16) + 128]
                        jobs.append((xe[128 * tt:128 * tt + 128, :], 128, gpre, "gpre", dst))
                    jobs.append((xs[:, :], 32, gpre, "gpre", hTs[:, :, :]))
                    for mt in range(2):
                        jobs.append((mem[128 * mt:128 * mt + 128, :], 128, gmem, "gmem", hTm[:, :, 128 * mt:128 * mt + 128]))
                    NLAG = 3
                    nsl = {}
                    for i in range(len(jobs) + NLAG):
                        if i < len(jobs):
                            nsl[i] = norm_a(jobs[i])
                        if i >= NLAG:
                            norm_b(jobs[i - NLAG], nsl[i - NLAG])
                        if i < 32 and i % 4 == 3:
                            table_chunk(i // 4)
                    S.op("pool", lambda e: e.tensor_copy(out=hTh32[:], in_=hTh[:, :, 2016:2048]), reads=["hT"], writes=["hTh32"])
                    S.barrier()
                    chk(1)

                with ExitStack() as p2:
                    chk(2)

                    kT = sbt(p2, "kT", [128, 4096], BF16)
                    qT = sbt(p2, "qT", [128, 2048], BF16)
                    vT = sbt(p2, "vT", [128, 4096], BF16)
                    kb = [sbt(p2, "kb%d" % i, [128, 512], BF16) for i in range(2)]
                    t1 = sbt(p2, "t1", [128, 512], F32)
                    t2 = sbt(p2, "t2", [128, 512], F32)
                    vx = sbt(p2, "vx", [128, 69, 3, 64], BF16)
                    vxf = vx[:].rearrange("p t b d -> p t (b d)")
                    acc = sbt(p2, "acc", [128, 2048], F32)
                    rz = sbt(p2, "rz", [128, 256], F32)
                    tmpn = sbt(p2, "tmpn", [128, 256], F32)
                    Eb = [sbt(p2, "Eb%d" % i, [128, 512], BF16) for i in range(4)]
                    ko32 = sbt(p2, "ko32", [128, 2, 128], F32)

                    S.op("pool", lambda e: e.memset(vx[:, :, 1, :], 1.0), writes=["vx1"])
                    halo_tiles = [0] + [17 + 5 * r for r in range(4)] + [37 + 2 * r for r in range(16)]
                    for ti in halo_tiles:
                        S.op("pool", lambda e, ti=ti: e.tensor_scalar(out=vx[:, ti, 1, :], in0=vx[:, ti, 1, :], scalar1=hfl[:, 0:1], scalar2=None, op0=ALU.mult),
                             reads=["vx1", "hfl"], writes=["vx1"])

                    def rope_seq(ws, groups, dst, dcol0):
                        st_ = {}

                        def stage_a(g):
                            pa = nxt("pA")
                            proj(ws, g, pa)
                            kbs = nxt("kb")
                            S.op("act", lambda e: e.copy(kb[kbs][:], PA(pa)[:]), reads=["pB%d" % pa], writes=["kb%d" % kbs])
                            st_[g] = (pa, kbs)

                        def stage_b(g):
                            pa, kbs = st_[g]
                            pb = nxt("pA")
                            S.op("pe", lambda e: e.matmul(PA(pb)[:], Rm[:], kb[kbs][:], start=True, stop=True), reads=["Rm", "kb%d" % kbs], writes=["pB%d" % pb])
                            tok = slice(512 * g, 512 * g + 512)
                            dcol = 512 * g - dcol0
                            S.op("dve", lambda e: e.tensor_tensor(out=t1[:], in0=PA(pa)[:], in1=cosT[:, tok], op=ALU.mult), reads=["pB%d" % pa, "cosT"], writes=["t1"])
                            S.op("dve", lambda e: e.tensor_tensor(out=t2[:], in0=PA(pb)[:], in1=sinT[:, tok], op=ALU.mult), reads=["pB%d" % pb, "sinT"], writes=["t2"])
                            S.op("dve", lambda e: e.tensor_tensor(out=dst[:, dcol:dcol + 512], in0=t1[:], in1=t2[:], op=ALU.add), reads=["t1", "t2"], writes=["qk"])

                        for i in range(len(groups) + 1):
                            if i < len(groups):
                                stage_a(groups[i])
                            if i >= 1:
                                stage_b(groups[i - 1])

                    for hp in range(3):
                        ws = wtile(384 + 128 * hp)
                        roles.update({"pA": [0, 1, 2, 3, 4, 5], "pT": [6, 7]})
                        rope_seq(ws, list(range(8)), kT, 0)
                        chk(2.1)
                        proj_s(ws, 384 + 128 * hp)
                        chk(2.2)
                        ws = wtile(128 * hp)
                        rope_seq(ws, [4, 5, 6, 7], qT, 2048)
                        proj_s(ws, 128 * hp)
                        ws = wtile(768 + 128 * hp)
                        for g in range(8):
                            pa = nxt("pA")
                            proj(ws, g, pa)
                            cp(vT[:, 512 * g:512 * g + 512], PA(pa)[:], ["pB%d" % pa], ["vT"])
                        proj_s(ws, 768 + 128 * hp)
                        chk(2.4)
                        ws = wtile(1152 + 128 * hp)
                        for g in range(4, 8):
                            pa = nxt("pA")
                            proj(ws, g, pa)
                            S.op("act", lambda e, g=g, pa=pa: e.activation(out=mixT[:, hp, 512 * (g - 4):512 * (g - 4) + 512], in_=PA(pa)[:], func=AF.Silu),
                                 reads=["pB%d" % pa], writes=["mix%d" % hp])
                        proj_s(ws, 1152 + 128 * hp)
                        chk(2.5)

                        tiles = []
                        for t in range(15, 32):
                            tiles.append((t - 15, slice(128 * t, 128 * t + 128)))
                        for r in range(4):
                            for J in range(3, 8):
                                tiles.append((17 + 5 * r + (J - 3), sl(512 * J + r, 128, 4)))
                        for r in range(16):
                            for J in range(2):
                                tiles.append((37 + 2 * r + J, sl(2048 * J + r, 128, 16)))
                        for b0 in range(0, 69, 8):
                            batch = tiles[b0:b0 + 8]
                            n = len(batch)
                            ps_ = nxt("pT")
                            for k_, (ti, tsl) in enumerate(batch):
                                S.op("pe", lambda e, k_=k_, tsl=tsl: e.transpose(PT(ps_)[:, 128 * k_:128 * k_ + 128], vT[:, tsl], ident[:]),
                                     reads=["vT", "ident"], writes=["pB%d" % ps_])
                            i0 = batch[0][0]
                            cp(vx[:, i0:i0 + n, 0:3:2, :], PT(ps_)[:, 0:128 * n].rearrange("p (t b d) -> p t b d", b=2, d=64),
                               ["pB%d" % ps_], ["vx"])

                        chk(2.6)
                        for b0 in range(0, 16, 2):
                            ps_ = nxt("pT")
                            for k_ in range(2):
                                t = 16 + b0 + k_
                                S.op("pe", lambda e, k_=k_, t=t: e.transpose(PT(ps_)[:, 128 * k_:128 * k_ + 128], kT[:, 128 * t:128 * t + 128], ident[:]),
                                     reads=["qk", "ident"], writes=["pB%d" % ps_])
                            cp(ko32[:].rearrange("p t f -> p (t f)"), PT(ps_)[:, 0:256], ["pB%d" % ps_], ["ko32"])
                            S.dma("sp", kp[128 * b0:128 * b0 + 256, 128 * hp:128 * hp + 128].rearrange("(t p) f -> p t f", p=128), ko32[:], reads=["ko32"])
                        chk(2.7)
                        for hh in range(2):
                            S.dma("pool", vp.rearrange("(t p) (h d) -> p t h d", p=128, d=64)[:, :, 2 * hp + hh, :], vx[:, 1:17, 2 * hh, :], reads=["vx"])

                        S.op("pool", lambda e: e.tensor_copy(out=vT[:, 0:2048].rearrange("p (r j) -> p r j", r=4), in_=qT[:, :].rearrange("p (j r) -> p r j", r=4)),
                             reads=["qk", "vT"], writes=["vT"])
                        S.op("pool", lambda e: e.tensor_copy(out=vT[:, 2048:4096].rearrange("p (r j) -> p r j", r=16), in_=qT[:, :].rearrange("p (j r) -> p r j", r=16)),
                             reads=["qk", "vT"], writes=["vT"])
                        chk(3)
                        roles.update({"pS": [0, 1, 2, 3], "pO": [4, 5, 6, 7]})
                        for hh in range(2):
                            r0 = 64 * hh
                            rows = slice(r0, r0 + 64)
                            zrows = slice(64 - r0, 128 - r0)
                            vcol = slice(64 * hh, 64 * hh + 128)

                            banks = []
                            for ob in range(4):
                                for b2 in range(2):
                                    t = 16 + 4 * ob + 2 * b2
                                    base = 256 * b2
                                    units = [
                                        (slice(128 * (t - 1), 128 * t), t - 1 - 15, qT[rows, 128 * (t - 16):128 * (t - 15)], 128, 0, base),
                                        (slice(128 * t, 128 * (t + 1)), t - 15, qT[rows, 128 * (t - 16):128 * (t - 14)], 256, 128, base),
                                        (slice(128 * (t + 1), 128 * (t + 2)), t + 1 - 15, qT[rows, 128 * (t - 15):128 * (t - 14)], 128, 384, base + 128),
                                    ]
                                    ev = (acc[:, 512 * ob:512 * ob + 512], None, True) if b2 == 1 else None
                                    banks.append((units, ob, b2 == 0, ev))
                            for r in range(4):
                                for b2 in range(2):
                                    J = 4 + 2 * b2
                                    base = 256 * b2
                                    vi = lambda Jk, r=r: 17 + 5 * r + (Jk - 3)
                                    units = [
                                        (sl(512 * (J - 1) + r, 128, 4), vi(J - 1), vT[rows, 512 * r + 128 * (J - 4):512 * r + 128 * (J - 3)], 128, 0, base),
                                        (sl(512 * J + r, 128, 4), vi(J), vT[rows, 512 * r + 128 * (J - 4):512 * r + 128 * (J - 2)], 256, 128, base),
                                        (sl(512 * (J + 1) + r, 128, 4), vi(J + 1), vT[rows, 512 * r + 128 * (J - 3):512 * r + 128 * (J - 2)], 128, 384, base + 128),
                                    ]
                                    ev = (acc[:, sl(r, 512, 4)], None, False) if b2 == 1 else None
                                    banks.append((units, 4 + r, b2 == 0, ev))
                            for ob in range(4):
                                for b2 in range(2):
                                    units = []
                                    for k_ in range(2):
                                        r = 4 * ob + 2 * b2 + k_
                                        oc0 = 128 * (2 * b2 + k_)
                                        for J in range(2):
                                            units.append((sl(2048 * J + r, 128, 16), 37 + 2 * r + J, vT[rows, 2048 + 128 * r:2048 + 128 * r + 128], 128, 256 * k_ + 128 * J, oc0))
                                    ev = (acc[:].rearrange("p (j r) -> p r j", r=16)[:, 4 * ob:4 * ob + 4, :], "rj", False) if b2 == 1 else None
                                    banks.append((units, 8 + ob, b2 == 0, ev))

                            LAG = 3
                            bst = {}
                            ost = {}

                            def emit_s(i):
                                units = banks[i][0]
                                s_ = nxt("pS")
                                eb = nxt("eb", 4)
                                bst[i] = (s_, eb)
                                for (ksl, ti, qsl, n, c0, oc0) in units:
                                    S.op("pe", lambda e, ksl=ksl, qsl=qsl, n=n, c0=c0: e.matmul(PA(s_)[:, c0:c0 + n], kT[rows, ksl], qsl, start=True, stop=True),
                                         reads=(["qk", "vT"] if i >= 8 else ["qk"]), writes=["pB%d" % s_])
                                S.op("act", lambda e: e.activation(out=Eb[eb][:], in_=PA(s_)[:], func=AF.Exp, scale=0.125), reads=["pB%d" % s_], writes=["Eb%d" % eb])
                                S.op("dve", lambda e: e.tensor_tensor(out=Eb[eb][:], in0=Eb[eb][:], in1=mask4[:], op=ALU.mult),
                                     reads=["Eb%d" % eb, "mask4"], writes=["Eb%d" % eb])

                            def emit_pv(i):
                                units, obid, first, ev = banks[i]
                                s_, eb = bst[i]
                                if first:
                                    ost[obid] = nxt("pO")
                                oslot = ost[obid]
                                for (ksl, ti, qsl, n, c0, oc0) in units:
                                    S.op("pe", lambda e, ti=ti, n=n, c0=c0, oc0=oc0, first=first: e.matmul(PA(oslot)[:, oc0:oc0 + n], vxf[:, ti, vcol], Eb[eb][:, c0:c0 + n],
                                                                                                        start=first, stop=False, skip_group_check=True),
                                         reads=["vx", "vx1", "Eb%d" % eb], writes=["pB%d" % oslot])
                                    first = False
                                if ev is not None:
                                    acc_ap, mode, first_dil = ev
                                    in_ap = PA(oslot)[:] if mode is None else PA(oslot)[:].rearrange("p (r j) -> p r j", j=128)
                                    if first_dil:
                                        S.op("act", lambda e: e.copy(acc_ap, in_ap), reads=["pB%d" % oslot], writes=["acc"])
                                    else:
                                        S.op("dve", lambda e: e.tensor_tensor(out=acc_ap, in0=in_ap, in1=acc_ap, op=ALU.add), reads=["pB%d" % oslot, "acc"], writes=["acc"])

                            for i in range(len(banks) + LAG):
                                if i < len(banks):
                                    emit_s(i)
                                if i >= LAG:
                                    emit_pv(i - LAG)
                            for pc in range(8):
                                cs = slice(256 * pc, 256 * pc + 256)
                                S.op("act", lambda e, cs=cs: e.activation(out=rz[rows, :], in_=acc[zrows, cs], func=AF.Ln), reads=["acc"], writes=["rz"])
                                S.op("act", lambda e: e.activation(out=rz[rows, :], in_=rz[rows, :], func=AF.Exp, scale=-1.0), reads=["rz"], writes=["rz"])
                                S.op("dve", lambda e, cs=cs: e.tensor_tensor(out=tmpn[rows, :], in0=acc[rows, cs], in1=rz[rows, :], op=ALU.mult),
                                     reads=["acc", "rz"], writes=["tmpn"])
                                S.op("pool", lambda e, cs=cs: e.tensor_tensor(out=mixT[rows, hp, cs], in0=tmpn[rows, :], in1=mixT[rows, hp, cs], op=ALU.mult),
                                     reads=["tmpn", "mix%d" % hp], writes=["mix%d" % hp])
                    S.barrier()

            roles.update({"pT": [0, 1], "pA": [2, 3], "pS": [4, 5], "pO": [6, 7]})
            chk(4)
            with ExitStack() as p3:
                ident32 = sbt(p3, "ident32", [128, 128], F32)
                ones32 = sbt(p3, "ones32", [128, 128], F32)
                S.op("pool", lambda e: e.memset(ones32[:], 1.0), writes=["ones32"])
                S.op("pool", lambda e: e.affine_select(out=ident32[:], in_=ones32[:], pattern=[[1, 128]], compare_op=ALU.is_equal,
                                                       fill=0.0, base=0, channel_multiplier=-1), reads=["ones32"], writes=["ident32"])
                qmT = sbt(p3, "qmT", [128, 2, 2048], BF16)
                sT = sbt(p3, "sT", [128, 19, 32], BF16)
                vnew = sbt(p3, "vnew", [8, 4, 384], BF16)
                qsz = sbt(p3, "qsz", [128, 6, 32], BF16)
                qmz = sbt(p3, "qmz", [128, 4, 32], BF16)
                rz = [sbt(p3, "rz2_%d" % i, [128, 512], F32) for i in range(2)]
                tmpn = [sbt(p3, "tmpn2_%d" % i, [128, 512], F32) for i in range(2)]
                Eb = [sbt(p3, "Ec%d" % i, [128, 512], BF16) for i in range(4)]

                with ExitStack() as p3c:
                    uT = sbt(p3c, "uT", [128, 3, 2080], BF16)
                    u32 = sbt(p3c, "u32", [128, 3, 32], F32)
                    sg = [sbt(p3c, "sg%d" % i, [128, 512], F32) for i in range(2)]
                    Dg = sbt(p3c, "Dg", [128, 31, 128], BF16)
                    wdw = sbt(p3c, "wdw", [128, 3, 31], F32)
                    wpw = sbt(p3c, "wpw", [128, 3, 384], BF16)
                    cT = sbt(p3c, "cT", [128, 3, 2048], F32)
                    cTs = sbt(p3c, "cTs", [128, 3, 32], F32)
                    cb = sbt(p3c, "cb", [128, 3, 512], BF16)
                    csq = sbt(p3c, "csq", [128, 3, 512], BF16)
                    mean = sbt(p3c, "mean", [128, 512], F32)
                    msq = sbt(p3c, "msq", [128, 512], F32)
                    var = sbt(p3c, "var", [128, 512], F32)
                    rs = sbt(p3c, "rs", [128, 512], F32)
                    dd = [sbt(p3c, "dd%d" % i, [128, 512], F32) for i in range(2)]
                    sw = sbt(p3c, "sw", [128, 3, 512], BF16)
                    ucT = sbt(p3c, "ucT", [128, 3, 4, 38], BF16)
                    scb = sbt(p3c, "scb", [30, 4, 384], BF16)
                    rcs_sb = sbt(p3c, "rcs_sb", [32, 49], F32)
                    sa = sbt(p3c, "sa", [32, 48], F32)
                    sb2 = sbt(p3c, "sb2", [32, 48], F32)
                    ski = sbt(p3c, "ski", [32, 48], I32)
                    skf = sbt(p3c, "skf", [32, 48], F32)
                    cos6 = sbt(p3c, "cos6", [32, 48], F32)
                    sin6 = sbt(p3c, "sin6", [32, 48], F32)
                    tA = sbt(p3c, "tA", [32, 48], F32)
                    tB = sbt(p3c, "tB", [32, 48], F32)
                    tC = sbt(p3c, "tC", [32, 48], F32)
                    tD = sbt(p3c, "tD", [32, 48], F32)
                    us32 = sbt(p3c, "us32", [32, 384], F32)
                    sgs = sbt(p3c, "sgs", [32, 384], F32)
                    tokb = sbt(p3c, "tokb", [32, 2816], BF16)
                    cvo = sbt(p3c, "cvo", [32, 384], F32)

                    roles.update({"pT": [0, 1], "pA": [2, 3, 4, 5, 6, 7]})
                    S.dma("sp", wdw[:], w_dwT[:], writes=["wdw"])
                    S.dma("pool", wpw[:], w_pw2.rearrange("(c p) n -> p c n", p=128), writes=["wpw"])
                    S.dma("sp", rcs_sb[:], rcs[:], writes=["rcs_sb"])
                    S.dma("pool", scb[:], sc.rearrange("b r n -> r b n"), writes=["scb"])

                    for cc in range(3):
                        wa = wtile(1536 + 128 * cc)
                        wb = wtile(1920 + 128 * cc)
                        for g in [3.5, 4, 5, 6, 7]:
                            pa, pb_ = nxt("pA"), nxt("pA")
                            if g == 3.5:
                                ntok, rf, dcol = 32, (lambda c: hTh32[:, c, :]), 0
                                proj(wa, None, pa, ntok, rf)
                                proj(wb, None, pb_, ntok, rf)
                            else:
                                ntok, dcol = 512, 32 + 512 * (g - 4)
                                proj(wa, g, pa)
                                proj(wb, g, pb_)
                            sgi = nxt("cp")
                            S.op("act", lambda e, pb_=pb_, ntok=ntok, sgi=sgi: e.activation(out=sg[sgi][:, 0:ntok], in_=PA(pb_)[:, 0:ntok], func=AF.Sigmoid),
                                 reads=["pB%d" % pb_], writes=["sg%d" % sgi])
                            S.op("dve", lambda e, pa=pa, ntok=ntok, dcol=dcol, sgi=sgi: e.tensor_tensor(out=uT[:, cc, dcol:dcol + ntok], in0=PA(pa)[:, 0:ntok],
                                                                                                    in1=sg[sgi][:, 0:ntok], op=ALU.mult),
                                 reads=["pB%d" % pa, "sg%d" % sgi], writes=["uT"])
                            if g == 7:
                                S.op("dve", lambda e, pa=pa, sgi=sgi: e.tensor_tensor(out=u32[:, cc, :], in0=PA(pa)[:, 480:512], in1=sg[sgi][:, 480:512], op=ALU.mult),
                                     reads=["pB%d" % pa, "sg%d" % sgi], writes=["u32"])
                        proj_s(wa, 1536 + 128 * cc)
                        proj_s(wb, 1920 + 128 * cc)
                        wg = wtile(2304 + 128 * cc)
                        for g in range(4, 8):
                            pa = nxt("pA")
                            proj(wg, g, pa)
                            S.op("act", lambda e, g=g, pa=pa: e.activation(out=mixT[:, 3 + cc, 512 * (g - 4):512 * (g - 4) + 512], in_=PA(pa)[:], func=AF.Silu),
                                 reads=["pB%d" % pa], writes=["mix%d" % (3 + cc)])
                        proj_s(wg, 2304 + 128 * cc)
                    for mp in range(2):
                        wq = wtile(2688 + 128 * mp)
                        for g in range(4, 8):
                            pa = nxt("pA")
                            proj(wq, g, pa)
                            cp(qmT[:, mp, 512 * (g - 4):512 * (g - 4) + 512], PA(pa)[:], ["pB%d" % pa], ["qmT"])
                        proj_s(wq, 2688 + 128 * mp)
                        wg = wtile(2944 + 128 * mp)
                        for g in range(4, 8):
                            pa = nxt("pA")
                            proj(wg, g, pa)
                            S.op("act", lambda e, g=g, pa=pa: e.activation(out=mixT[:, 6 + mp, 512 * (g - 4):512 * (g - 4) + 512], in_=PA(pa)[:], func=AF.Silu),
                                 reads=["pB%d" % pa], writes=["mix%d" % (6 + mp)])
                        proj_s(wg, 2944 + 128 * mp)
                    pa = nxt("pA")
                    for cc in range(3):
                        S.op("pe", lambda e, cc=cc: e.transpose(PA(pa)[0:32, 128 * cc:128 * cc + 128], u32[:, cc, :], ident32[:]),
                             reads=["u32", "ident32"], writes=["pB%d" % pa])
                    cp(cvo[:], PA(pa)[0:32, 0:384], ["pB%d" % pa], ["cvo"])
                    S.dma("sp", cvp[:, :], cvo[2:32, :], reads=["cvo"])
                    chk(4.1)

                    S.op("dve", lambda e: e.tensor_scalar(out=sa[:], in0=rcs_sb[:, 1:49], scalar1=rcs_sb[:, 0:1], scalar2=None, op0=ALU.mult),
                         reads=["rcs_sb"], writes=["sa"])
                    for which in range(2):
                        src, sname = sa, "sa"
                        if which == 1:
                            S.op("dve", lambda e: e.tensor_scalar(out=sb2[:], in0=sa[:], scalar1=math.pi / 2, scalar2=None, op0=ALU.add), reads=["sa"], writes=["sb2"])
                            src, sname = sb2, "sb2"
                        S.op("dve", lambda e, src=src: e.tensor_scalar(out=ski[:], in0=src[:], scalar1=1.0 / TWO_PI, scalar2=None, op0=ALU.mult), reads=[sname], writes=["ski"])
                        S.op("dve", lambda e: e.tensor_copy(out=skf[:], in_=ski[:]), reads=["ski"], writes=["skf"])
                        S.op("dve", lambda e, src=src: e.scalar_tensor_tensor(out=src[:], in0=skf[:], scalar=-C1, in1=src[:], op0=ALU.mult, op1=ALU.add), reads=["skf", sname], writes=[sname])
                        S.op("dve", lambda e, src=src: e.scalar_tensor_tensor(out=src[:], in0=skf[:], scalar=-C2, in1=src[:], op0=ALU.mult, op1=ALU.add), reads=["skf", sname], writes=[sname])
                        S.op("dve", lambda e, src=src: e.tensor_scalar(out=src[:], in0=src[:], scalar1=-PI_CL, scalar2=PI_CL, op0=ALU.max, op1=ALU.min), reads=[sname], writes=[sname])
                        dstt, dn = (sin6, "sin6") if which == 0 else (cos6, "cos6")
                        S.op("act", lambda e, src=src, dstt=dstt: e.activation(out=dstt[:], in_=src[:], func=AF.Sin), reads=[sname], writes=[dn])
                    c6 = cos6[:].rearrange("p (h d) -> p h d", d=8)
                    s6 = sin6[:].rearrange("p (h d) -> p h d", d=8)
                    v3 = lambda t: t[:].rearrange("p (h d) -> p h d", d=8)
                    for base in (0, 384):
                        qv = ts32[:, base:base + 384].rearrange("p (h d) -> p h d", d=64)
                        x1, x2 = qv[:, :, 0:8], qv[:, :, 8:16]
                        S.op("dve", lambda e, x1=x1: e.tensor_tensor(out=v3(tA), in0=x1, in1=c6, op=ALU.mult), reads=["ts32", "cos6"], writes=["tA"])
                        S.op("dve", lambda e, x2=x2: e.tensor_tensor(out=v3(tB), in0=x2, in1=s6, op=ALU.mult), reads=["ts32", "sin6"], writes=["tB"])
                        S.op("dve", lambda e, x2=x2: e.tensor_tensor(out=v3(tC), in0=x2, in1=c6, op=ALU.mult), reads=["ts32", "cos6"], writes=["tC"])
                        S.op("dve", lambda e, x1=x1: e.tensor_tensor(out=v3(tD), in0=x1, in1=s6, op=ALU.mult), reads=["ts32", "sin6"], writes=["tD"])
                        S.op("dve", lambda e, x1=x1: e.tensor_tensor(out=x1, in0=v3(tA), in1=v3(tB), op=ALU.subtract), reads=["tA", "tB"], writes=["ts32"])
                        S.op("dve", lambda e, x2=x2: e.tensor_tensor(out=x2, in0=v3(tC), in1=v3(tD), op=ALU.add), reads=["tC", "tD"], writes=["ts32"])
                    S.op("act", lambda e: e.activation(out=sgs[:], in_=ts32[:, 1920:2304], func=AF.Sigmoid), reads=["ts32"], writes=["sgs"])
                    S.op("dve", lambda e: e.tensor_tensor(out=us32[:], in0=ts32[:, 1536:1920], in1=sgs[:], op=ALU.mult), reads=["ts32", "sgs"], writes=["us32"])
                    for bl in range(4):
                        S.dma("sp", ksn[bl, 2040:2048, :], ts32[8 * bl:8 * bl + 8, 384:768], reads=["ts32"])
                        S.dma("sp", vsn[bl, 2040:2048, :], ts32[8 * bl:8 * bl + 8, 768:1152], reads=["ts32"])
                        S.dma("sp", csn[bl, 22:30, :], us32[8 * bl:8 * bl + 8, :], reads=["us32"])
                    S.op("dve", lambda e: e.tensor_copy(out=tokb[:, 0:768], in_=ts32[:, 0:768]), reads=["ts32"], writes=["tokb"])
                    S.op("dve", lambda e: e.tensor_copy(out=tokb[:, 1024:1408], in_=us32[:]), reads=["us32", "tokb"], writes=["tokb"])
                    S.op("dve", lambda e: e.tensor_copy(out=tokb[:, 768:1024], in_=ts32[:, 2688:2944]), reads=["ts32", "tokb"], writes=["tokb"])
                    S.op("act", lambda e: e.activation(out=tokb[:, 1408:1792], in_=ts32[:, 1152:1536], func=AF.Silu), reads=["ts32", "tokb"], writes=["tokb"])
                    S.op("act", lambda e: e.activation(out=tokb[:, 1792:2176], in_=ts32[:, 2304:2688], func=AF.Silu), reads=["ts32", "tokb"], writes=["tokb"])
                    S.op("act", lambda e: e.activation(out=tokb[:, 2176:2432], in_=ts32[:, 2944:3200], func=AF.Silu), reads=["ts32", "tokb"], writes=["tokb"])
                    S.op("dve", lambda e: e.tensor_copy(out=tokb[:, 2432:2816], in_=ts32[:, 768:1152]), reads=["ts32", "tokb"], writes=["tokb"])
                    ps_ = nxt("pT")
                    for k_ in range(19):
                        S.op("pe", lambda e, k_=k_: e.transpose(PT(ps_)[:, 32 * k_:32 * k_ + 32], tokb[:, 128 * k_:128 * k_ + 128], ident[0:32, 0:32]),
                             reads=["tokb", "ident"], writes=["pB%d" % ps_])
                    cp(sT[:].rearrange("p k t -> p (k t)"), PT(ps_)[:, 0:608], ["pB%d" % ps_], ["sT"])
                    S.op("pool", lambda e: e.memset(qsz[:], 0.0), writes=["qsz"])
                    S.op("pool", lambda e: e.memset(qmz[:], 0.0), writes=["qmz"])
                    for h_ in range(6):
                        hr_ = slice(64 * (h_ % 2), 64 * (h_ % 2) + 64)
                        S.op("pool", lambda e, h_=h_, hr_=hr_: e.tensor_copy(out=qsz[hr_, h_, :], in_=sT[hr_, h_ // 2, :]), reads=["sT", "qsz"], writes=["qsz"])
                    for h_ in range(4):
                        hr_ = slice(64 * (h_ % 2), 64 * (h_ % 2) + 64)
                        S.op("pool", lambda e, h_=h_, hr_=hr_: e.tensor_copy(out=qmz[hr_, h_, :], in_=sT[hr_, 6 + h_ // 2, :]), reads=["sT", "qmz"], writes=["qmz"])
                    pa = nxt("pA")
                    for bl in range(4):
                        pass
                    for bl in range(4):
                        pa = nxt("pA")
                        S.op("pe", lambda e, bl=bl, pa=pa: e.matmul(PA(pa)[0:8, 0:384], ident[0:32, 8 * bl:8 * bl + 8], tokb[:, 2432:2816], start=True, stop=True),
                             reads=["tokb", "ident"], writes=["pB%d" % pa])
                        cp(vnew[:, bl, :], PA(pa)[0:8, 0:384], ["pB%d" % pa], ["vnew"])
                    ps_ = nxt("pT")
                    for cc in range(3):
                        for bl in range(4):
                            k_ = 4 * cc + bl
                            S.op("pe", lambda e, k_=k_, cc=cc, bl=bl: e.transpose(PT(ps_)[:, 32 * k_:32 * k_ + 30], scb[:, bl, 128 * cc:128 * cc + 128], ident[0:30, 0:30]),
                                 reads=["scb", "ident"], writes=["pB%d" % ps_])
                    cp(ucT[:].rearrange("p c b t -> p (c b) t")[:, :, 0:30], PT(ps_)[:, 0:384].rearrange("p (k t) -> p k t", t=32)[:, :, 0:30],
                       ["pB%d" % ps_], ["ucT"])
                    S.op("dve", lambda e: e.tensor_copy(out=ucT[:, :, :, 30:38], in_=sT[:, 8:11, :].rearrange("p c (b t) -> p c b t", t=8)),
                         reads=["sT", "ucT"], writes=["ucT"])
                    chk(4.2)

                    for cc in range(3):
                        for j in range(31):
                            S.op("dve", lambda e, j=j: e.tensor_scalar(out=Dg[:, j, :], in0=ident[:], scalar1=wdw[:, cc, j:j + 1], scalar2=None, op0=ALU.mult),
                                 reads=["ident", "wdw", "Dg"], writes=["Dg"])
                        for g in range(4):
                            pa = nxt("pA")
                            for j in range(31):
                                S.op("pe", lambda e, j=j, g=g, pa=pa: e.matmul(PA(pa)[:], Dg[:, j, :], uT[:, cc, 512 * g + j + 2:512 * g + j + 2 + 512],
                                                                               start=(j == 0), stop=(j == 30)),
                                     reads=["Dg", "uT"], writes=["pB%d" % pa])
                            S.op("act", lambda e, g=g, pa=pa: e.activation(out=cT[:, cc, 512 * g:512 * g + 512], in_=PA(pa)[:], func=AF.Identity, bias=vec_sb[:, 0, cc:cc + 1]),
                                 reads=["pB%d" % pa, "vec_sb"], writes=["cT"])
                        pa = nxt("pA")
                        for j in range(31):
                            S.op("pe", lambda e, j=j, pa=pa: e.matmul(PA(pa)[:, 0:32], Dg[:, j, :], ucT[:, cc, :, j:j + 8], start=(j == 0), stop=(j == 30)),
                                 reads=["Dg", "ucT"], writes=["pB%d" % pa])
                        S.op("act", lambda e, pa=pa: e.activation(out=cTs[:, cc, :], in_=PA(pa)[:, 0:32], func=AF.Identity, bias=vec_sb[:, 0, cc:cc + 1]),
                             reads=["pB%d" % pa, "vec_sb"], writes=["cTs"])
                    chk(4.3)

                    mean2 = [mean, tmpn[1]]
                    rs2_ = [rs, rz[1]]

                    def conf_front(k, cap, cname, N):
                        mn, rsx = mean2[k % 2], rs2_[k % 2]
                        mname, rname = "mean%d" % (k % 2), "rs%d" % (k % 2)
                        for cc in range(3):
                            S.op("pool", lambda e, cc=cc: e.tensor_copy(out=cb[:, cc, 0:N], in_=cap(cc)), reads=[cname, "cb"], writes=["cb"])
                            S.op("act", lambda e, cc=cc: e.activation(out=csq[:, cc, 0:N], in_=cap(cc), func=AF.Square), reads=[cname, "csq"], writes=["csq"])
                        pm, pq = nxt("pA"), nxt("pA")
                        for cc in range(3):
                            S.op("pe", lambda e, cc=cc: e.matmul(PA(pm)[:, 0:N], onesb[:], cb[:, cc, 0:N], start=(cc == 0), stop=(cc == 2)),
                                 reads=["onesb", "cb"], writes=["pB%d" % pm])
                        for cc in range(3):
                            S.op("pe", lambda e, cc=cc: e.matmul(PA(pq)[:, 0:N], onesb[:], csq[:, cc, 0:N], start=(cc == 0), stop=(cc == 2)),
                                 reads=["onesb", "csq"], writes=["pB%d" % pq])
                        S.op("dve", lambda e: e.tensor_scalar(out=mn[:, 0:N], in0=PA(pm)[:, 0:N], scalar1=1.0 / 384, scalar2=None, op0=ALU.mult),
                             reads=["pB%d" % pm], writes=[mname])
                        S.op("dve", lambda e: e.tensor_tensor(out=msq[:, 0:N], in0=mn[:, 0:N], in1=mn[:, 0:N], op=ALU.mult), reads=[mname], writes=["msq"])
                        S.op("dve", lambda e: e.scalar_tensor_tensor(out=var[:, 0:N], in0=PA(pq)[:, 0:N], scalar=1.0 / 384, in1=msq[:, 0:N], op0=ALU.mult, op1=ALU.subtract),
                             reads=["pB%d" % pq, "msq"], writes=["var"])
                        S.op("act", lambda e: e.activation(out=var[:, 0:N], in_=var[:, 0:N], func=AF.Ln, bias=epst[:, 0:1]), reads=["var", "epst"], writes=["var"])
                        S.op("act", lambda e: e.activation(out=rsx[:, 0:N], in_=var[:, 0:N], func=AF.Exp, scale=-0.5), reads=["var"], writes=[rname])

                    def conf_back(k, cap, cname, N, out_fn):
                        mn, rsx = mean2[k % 2], rs2_[k % 2]
                        mname, rname = "mean%d" % (k % 2), "rs%d" % (k % 2)
                        for cc in range(3):
                            di = nxt("cp")
                            S.op("dve", lambda e, cc=cc, di=di: e.tensor_tensor(out=dd[di][:, 0:N], in0=cap(cc), in1=mn[:, 0:N], op=ALU.subtract),
                                 reads=[cname, mname], writes=["dd%d" % di])
                            S.op("dve", lambda e, di=di: e.tensor_tensor(out=dd[di][:, 0:N], in0=dd[di][:, 0:N], in1=rsx[:, 0:N], op=ALU.mult),
                                 reads=["dd%d" % di, rname], writes=["dd%d" % di])
                            S.op("act", lambda e, cc=cc, di=di: e.activation(out=sw[:, cc, 0:N], in_=dd[di][:, 0:N], func=AF.Silu,
                                                                             scale=vec_sb[:, 1, cc:cc + 1], bias=vec_sb[:, 2, cc:cc + 1]),
                                 reads=["dd%d" % di, "vec_sb", "sw"], writes=["sw"])
                        for fo in range(3):
                            po = nxt("pA")
                            for cc in range(3):
                                S.op("pe", lambda e, cc=cc, fo=fo, po=po: e.matmul(PA(po)[:, 0:N], wpw[:, cc, 128 * fo:128 * fo + 128], sw[:, cc, 0:N],
                                                                                   start=(cc == 0), stop=(cc == 2)),
                                     reads=["wpw", "sw"], writes=["pB%d" % po])
                            out_fn(fo, po)

                    def mk_out_p(g):
                        def out_p(fo, po):
                            dst = mixT[:, 3 + fo, 512 * g:512 * g + 512]
                            S.op("dve", lambda e: e.scalar_tensor_tensor(out=dst, in0=PA(po)[:], scalar=vec_sb[:, 3, fo:fo + 1], in1=dst, op0=ALU.add, op1=ALU.mult),
                                 reads=["pB%d" % po, "vec_sb", "mix%d" % (3 + fo)], writes=["mix%d" % (3 + fo)])
                        return out_p

                    def out_s(fo, po):
                        S.op("dve", lambda e: e.scalar_tensor_tensor(out=mixTs[:, 3 + fo, :], in0=PA(po)[:, 0:32], scalar=vec_sb[:, 3, fo:fo + 1], in1=sT[:, 14 + fo, :],
                                                                     op0=ALU.add, op1=ALU.mult),
                             reads=["pB%d" % po, "vec_sb", "sT"], writes=["mixTs"])

                    cjobs = [((lambda cc, g=g: cT[:, cc, 512 * g:512 * g + 512]), "cT", 512, mk_out_p(g)) for g in range(4)]
                    cjobs.append(((lambda cc: cTs[:, cc, :]), "cTs", 32, out_s))
                    for k in range(len(cjobs) + 1):
                        if k < len(cjobs):
                            conf_front(k, cjobs[k][0], cjobs[k][1], cjobs[k][2])
                        if k >= 1:
                            j = cjobs[k - 1]
                            conf_back(k - 1, j[0], j[1], j[2], j[3])
                    roles.update({"pT": [0, 1], "pA": [2, 3], "pS": [4, 5], "pO": [6, 7]})
                    S.barrier()
                chk(5)

                with ExitStack() as p4:
                    wm = sbt(p4, "wm", [128, 8, 512], BF16)
                    kmT = sbt(p4, "kmT", [128, 2, 256], BF16)
                    vxm = sbt(p4, "vxm", [128, 2, 2, 3, 64], BF16)
                    vxmf = vxm[:].rearrange("p t m b d -> p t m (b d)")
                    mkv32 = sbt(p4, "mkv32", [128, 2, 512], F32)
                    cmkb = sbt(p4, "cmkb", [128, 2, 256], BF16)
                    cmvb = sbt(p4, "cmvb", [128, 2, 256], BF16)
                    kmsT = sbt(p4, "kmsT", [128, 2, 256], BF16)
                    Es = sbt(p4, "Es", [128, 16], BF16)
                    S.dma("pool", wm[:], w_mkv.rearrange("(c p) n -> p c n", p=128), writes=["wm"])
                    S.op("pool", lambda e: e.memset(vxm[:, :, :, 1, :], 1.0), writes=["vxm1"])
                    for mt in range(2):
                        pa = nxt("pA")
                        for c in range(8):
                            S.op("pe", lambda e, c=c, mt=mt, pa=pa: e.matmul(PA(pa)[:], hTm[:, c, 128 * mt:128 * mt + 128], wm[:, c, :], start=(c == 0), stop=(c == 7)),
                                 reads=["wm", "hT"], writes=["pB%d" % pa])
                        cp(mkv32[:, mt, :], PA(pa)[:], ["pB%d" % pa], ["mkv32"])
                        S.dma("sp", mkp[128 * mt:128 * mt + 128, :], mkv32[:, mt, 0:256], reads=["mkv32"])
                        S.dma("sp", mvp[128 * mt:128 * mt + 128, :], mkv32[:, mt, 256:512], reads=["mkv32"])
                        for mp in range(2):
                            S.op("pool", lambda e, mt=mt, mp=mp: e.tensor_copy(out=vxm[:, mt, mp, 0:3:2, :],
                                                                               in_=mkv32[:, mt, 256 + 128 * mp:256 + 128 * mp + 128].rearrange("p (b d) -> p b d", d=64)),
                                 reads=["mkv32"], writes=["vxm"])
                    for mp in range(2):
                        pa = nxt("pA")
                        for c in range(8):
                            S.op("pe", lambda e, c=c, mp=mp, pa=pa: e.matmul(PA(pa)[:, 0:256], wm[:, c, 128 * mp:128 * mp + 128], hTm[:, c, :], start=(c == 0), stop=(c == 7)),
                                 reads=["wm", "hT"], writes=["pB%d" % pa])
                        cp(kmT[:, mp, :], PA(pa)[:, 0:256], ["pB%d" % pa], ["kmT"])
                    roles.update({"pS": [0, 1, 2, 3], "pO": [4, 5, 6, 7]})
                    its = [(mh, g) for mh in range(4) for g in range(4)]
                    mst = {}

                    def mem_s(i):
                        mh, g = its[i]
                        mp, hh = mh // 2, mh % 2
                        rows = slice(64 * hh, 64 * hh + 64)
                        cs = slice(512 * g, 512 * g + 512)
                        sl_ = []
                        for kt in range(2):
                            s_ = nxt("pS")
                            eb = nxt("eb", 4)
                            S.op("pe", lambda e, kt=kt, s_=s_: e.matmul(PA(s_)[:], kmT[rows, mp, 128 * kt:128 * kt + 128], qmT[rows, mp, cs], start=True, stop=True),
                                 reads=["kmT", "qmT"], writes=["pB%d" % s_])
                            S.op("act", lambda e, s_=s_, eb=eb: e.activation(out=Eb[eb][:], in_=PA(s_)[:], func=AF.Exp, scale=0.125), reads=["pB%d" % s_], writes=["Ec%d" % eb])
                            sl_.append(eb)
                        mst[i] = sl_

                    def mem_pv(i):
                        mh, g = its[i]
                        mp, hh = mh // 2, mh % 2
                        r0 = 64 * hh
                        rows = slice(r0, r0 + 64)
                        zrows = slice(64 - r0, 128 - r0)
                        vcol = slice(64 * hh, 64 * hh + 128)
                        cs = slice(512 * g, 512 * g + 512)
                        oslot = nxt("pO")
                        for kt in range(2):
                            eb = mst[i][kt]
                            S.op("pe", lambda e, kt=kt, eb=eb: e.matmul(PA(oslot)[:], vxmf[:, kt, mp, vcol], Eb[eb][:], start=(kt == 0), stop=(kt == 1), skip_group_check=True),
                                 reads=["vxm", "vxm1", "Ec%d" % eb], writes=["pB%d" % oslot])
                        rzs = nxt("rzs")
                        S.op("act", lambda e: e.activation(out=rz[rzs][rows, :], in_=PA(oslot)[zrows, :], func=AF.Ln), reads=["pB%d" % oslot], writes=["rz2_%d" % rzs])
                        S.op("act", lambda e: e.activation(out=rz[rzs][rows, :], in_=rz[rzs][rows, :], func=AF.Exp, scale=-1.0), reads=["rz2_%d" % rzs], writes=["rz2_%d" % rzs])
                        S.op("dve", lambda e: e.tensor_tensor(out=tmpn[rzs][rows, :], in0=PA(oslot)[rows, :], in1=rz[rzs][rows, :], op=ALU.mult),
                             reads=["pB%d" % oslot, "rz2_%d" % rzs], writes=["tmpn2_%d" % rzs])
                        S.op("pool", lambda e: e.tensor_tensor(out=mixT[rows, 6 + mp, cs], in0=tmpn[rzs][rows, :], in1=mixT[rows, 6 + mp, cs], op=ALU.mult),
                             reads=["tmpn2_%d" % rzs, "mix%d" % (6 + mp)], writes=["mix%d" % (6 + mp)])

                    for i in range(len(its) + 1):
                        if i < len(its):
                            mem_s(i)
                        if i >= 1:
                            mem_pv(i - 1)

                    chk(5.5)
                    roles.update({"pT": [0, 1], "pS": [2, 3], "pO": [4, 5]})
                    Onm = sbt(p4, "Onm", [128, 2, 32], F32)
                    zsm = [sbt(p4, "zsm%d" % i, [128, 4, 8], F32) for i in range(2)]
                    Es2 = [sbt(p4, "Es2_%d" % i, [128, 64], BF16) for i in range(2)]
                    cmkb2 = [cmkb, sbt(p4, "cmkb_1", [128, 2, 256], BF16)]
                    cmvb2 = [cmvb, sbt(p4, "cmvb_1", [128, 2, 256], BF16)]
                    kmsT2 = [kmsT, sbt(p4, "kmsT_1", [128, 2, 256], BF16)]
                    def load_m(bl):
                        sl_ = bl % 2
                        S.dma("pool", cmkb2[sl_][:], cmk[bl].rearrange("(t p) n -> p t n", p=128), writes=["cmkb%d" % sl_])
                        S.dma("pool", cmvb2[sl_][:], cmv[bl].rearrange("(t p) n -> p t n", p=128), writes=["cmvb%d" % sl_])

                    load_m(0)
                    load_m(1)
                    for bl in range(4):
                        sl_ = bl % 2
                        ps_ = nxt("pT")
                        for kt in range(2):
                            for mp in range(2):
                                k_ = 2 * mp + kt
                                S.op("pe", lambda e, kt=kt, mp=mp, k_=k_: e.transpose(PT(ps_)[:, 128 * k_:128 * k_ + 128], cmkb2[sl_][:, kt, 128 * mp:128 * mp + 128], ident[:]),
                                     reads=["cmkb%d" % sl_, "ident"], writes=["pB%d" % ps_])
                        cp(kmsT2[sl_][:].rearrange("p m j -> p (m j)"), PT(ps_)[:, 0:512], ["pB%d" % ps_], ["kmsT%d" % sl_])
                        s_ = nxt("pS")
                        oslot = nxt("pO")
                        for mh in range(4):
                            mp, hh = mh // 2, mh % 2
                            rows = slice(64 * hh, 64 * hh + 64)
                            qs = qmz[:, mh, 8 * bl:8 * bl + 8]
                            for kt in range(2):
                                S.op("pe", lambda e, kt=kt, mh=mh, mp=mp, rows=rows, qs=qs: e.matmul(PA(s_)[:, 32 * kt + 8 * mh:32 * kt + 8 * mh + 8],
                                                                                                  kmsT2[sl_][:, mp, 128 * kt:128 * kt + 128], qs, start=True, stop=True),
                                     reads=["kmsT%d" % sl_, "qmz"], writes=["pB%d" % s_])
                        S.op("act", lambda e: e.activation(out=Es2[sl_][:], in_=PA(s_)[:, 0:64], func=AF.Exp, scale=0.125), reads=["pB%d" % s_], writes=["Es2_%d" % sl_])
                        for mh in range(4):
                            hh = mh % 2
                            rows = slice(64 * hh, 64 * hh + 64)
                            for kt in range(2):
                                S.op("pe", lambda e, kt=kt, mh=mh, rows=rows: e.matmul(PA(oslot)[rows, 8 * mh:8 * mh + 8], cmvb2[sl_][:, kt, 64 * mh:64 * mh + 64],
                                                                                      Es2[sl_][:, 32 * kt + 8 * mh:32 * kt + 8 * mh + 8],
                                                                                      start=(kt == 0), stop=(kt == 1), skip_group_check=True),
                                     reads=["cmvb%d" % sl_, "Es2_%d" % sl_], writes=["pB%d" % oslot])
                        for hf in range(2):
                            hr = slice(64 * hf, 64 * hf + 64)
                            for kt in range(2):
                                S.op("pe", lambda e, kt=kt, hr=hr: e.matmul(PA(oslot)[hr, 64:96], onesb[:, 0:64], Es2[sl_][:, 32 * kt:32 * kt + 32], start=(kt == 0), stop=(kt == 1), skip_group_check=True),
                                     reads=["onesb", "Es2_%d" % sl_], writes=["pB%d" % oslot])
                        for hf in range(2):
                            hr = slice(64 * hf, 64 * hf + 64)
                            S.op("dve", lambda e, hr=hr: e.reciprocal(out=zsm[sl_][hr].rearrange("p h q -> p (h q)"), in_=PA(oslot)[hr, 64:96]), reads=["pB%d" % oslot, "zsm%d" % sl_], writes=["zsm%d" % sl_])
                        for mh in range(4):
                            mp, hh = mh // 2, mh % 2
                            rows = slice(64 * hh, 64 * hh + 64)
                            S.op("dve", lambda e, mh=mh, mp=mp, rows=rows: e.tensor_tensor(out=Onm[rows, mp, 8 * bl:8 * bl + 8], in0=PA(oslot)[rows, 8 * mh:8 * mh + 8],
                                                                                         in1=zsm[sl_][rows, mh, :], op=ALU.mult),
                                 reads=["pB%d" % oslot, "zsm%d" % sl_, "Onm"], writes=["Onm"])
                        if bl + 2 < 4:
                            load_m(bl + 2)
                    S.op("dve", lambda e: e.tensor_tensor(out=mixTs[:, 6:8, :], in0=Onm[:], in1=sT[:, 17:19, :], op=ALU.mult), reads=["Onm", "sT"], writes=["mixTs"])
                    S.barrier()
                chk(6)

                with ExitStack() as p6:
                    ckb = sbt(p6, "ckb", [128, 16, 384], BF16)
                    cvb = sbt(p6, "cvb", [128, 16, 384], BF16)
                    ksT = sbt(p6, "ksT", [128, 3, 2056], BF16)
                    Cm = sbt(p6, "Cm", [128, 17, 8], BF16)
                    di_ = sbt(p6, "di_", [128, 136], I32)
                    da_ = sbt(p6, "da_", [128, 136], I32)
                    df_ = sbt(p6, "df_", [128, 136], F32)
                    m1_ = sbt(p6, "m1_", [128, 136], F32)
                    m2_ = sbt(p6, "m2_", [128, 136], F32)
                    m3_ = sbt(p6, "m3_", [128, 136], F32)
                    Ew = sbt(p6, "Ew", [128, 136], BF16)
                    zs = sbt(p6, "zs", [128, 8], F32)
                    S.op("pool", lambda e: e.iota(di_[:], pattern=[[128, 17], [-1, 8]], base=0, channel_multiplier=1), writes=["di_"])
                    S.op("dve", lambda e: e.tensor_copy(out=df_[:], in_=di_[:]), reads=["di_"], writes=["df_"])
                    S.op("dve", lambda e: e.tensor_scalar(out=m1_[:], in0=df_[:], scalar1=1920.0, scalar2=None, op0=ALU.is_ge), reads=["df_"], writes=["m1_"])
                    S.op("dve", lambda e: e.tensor_single_scalar(out=da_[:], in_=di_[:], scalar=3, op=ALU.bitwise_and), reads=["di_"], writes=["da_"])
                    S.op("dve", lambda e: e.tensor_scalar(out=m2_[:], in0=da_[:], scalar1=0.0, scalar2=None, op0=ALU.is_equal), reads=["da_"], writes=["m2_"])
                    S.op("dve", lambda e: e.tensor_scalar(out=m3_[:], in0=df_[:], scalar1=1536.0, scalar2=None, op0=ALU.is_ge), reads=["df_"], writes=["m3_"])
                    S.op("dve", lambda e: e.tensor_tensor(out=m2_[:], in0=m2_[:], in1=m3_[:], op=ALU.mult), reads=["m2_", "m3_"], writes=["m2_"])
                    S.op("dve", lambda e: e.tensor_tensor(out=m1_[:], in0=m1_[:], in1=m2_[:], op=ALU.add), reads=["m1_", "m2_"], writes=["m1_"])
                    S.op("dve", lambda e: e.tensor_single_scalar(out=da_[:], in_=di_[:], scalar=15, op=ALU.bitwise_and), reads=["di_", "m2_"], writes=["da_"])
                    S.op("dve", lambda e: e.tensor_scalar(out=m2_[:], in0=da_[:], scalar1=0.0, scalar2=None, op0=ALU.is_equal), reads=["da_"], writes=["m2_"])
                    S.op("dve", lambda e: e.tensor_tensor(out=m1_[:], in0=m1_[:], in1=m2_[:], op=ALU.add), reads=["m1_", "m2_"], writes=["m1_"])
                    S.op("dve", lambda e: e.tensor_scalar(out=m3_[:], in0=df_[:], scalar1=2048.0, scalar2=None, op0=ALU.is_le), reads=["df_", "m2_"], writes=["m3_"])
                    S.op("dve", lambda e: e.tensor_tensor(out=Cm[:].rearrange("p k q -> p (k q)"), in0=m1_[:], in1=m3_[:], op=ALU.mult), reads=["m1_", "m3_"], writes=["Cm"])
                    Cmf = Cm[:].rearrange("p k q -> p (k q)")
                    Cm3 = sbt(p6, "Cm3", [128, 17, 3, 8], BF16)
                    for h3 in range(3):
                        S.op("dve", lambda e, h3=h3: e.tensor_copy(out=Cm3[:, :, h3, :], in_=Cm[:]), reads=["Cm", "Cm3"], writes=["Cm3"])
                    Cm3f = Cm3[:].rearrange("p k h q -> p (k h q)")
                    roles.update({"pT": [0, 1, 2, 3], "pS": [4, 5], "pO": [6, 7]})
                    for bk in roles["pS"]:
                        S.op("dve", lambda e, bk=bk: e.memset(PA(bk)[:], 0.0), writes=["pB%d" % bk])
                    ckb2 = [ckb, sbt(p6, "ckb_1", [128, 16, 384], BF16)]
                    cvb2 = [cvb, sbt(p6, "cvb_1", [128, 16, 384], BF16)]
                    ksT2 = [ksT, sbt(p6, "ksT_1", [128, 3, 2056], BF16)]
                    Ew2 = [sbt(p6, "Ew_%d" % i, [128, 408], BF16) for i in range(2)]
                    zs2 = [sbt(p6, "zs_%d" % i, [128, 3, 8], F32) for i in range(2)]
                    On = sbt(p6, "On", [128, 3, 32], F32)

                    def load_bl(bl):
                        sl_ = bl % 2
                        S.dma("pool", ckb2[sl_][:], ck[bl].rearrange("(t p) n -> p t n", p=128), writes=["ckb%d" % sl_])
                        S.dma("pool", cvb2[sl_][:], cv[bl].rearrange("(t p) n -> p t n", p=128), writes=["cvb%d" % sl_])

                    load_bl(0)
                    load_bl(1)
                    for bl in range(4):
                        sl_ = bl % 2
                        for ft in range(3):
                            for b0 in range(0, 16, 8):
                                ps_ = nxt("pT")
                                for k_ in range(8):
                                    S.op("pe", lambda e, k_=k_, b0=b0, ft=ft: e.transpose(PT(ps_)[:, 128 * k_:128 * k_ + 128], ckb2[sl_][:, b0 + k_, 128 * ft:128 * ft + 128], ident[:]),
                                         reads=["ckb%d" % sl_, "ident"], writes=["pB%d" % ps_])
                                cp(ksT2[sl_][:, ft, 128 * b0:128 * b0 + 1024], PT(ps_)[:, :], ["pB%d" % ps_], ["ksT%d" % sl_])
                        S.op("dve", lambda e, bl=bl: e.tensor_copy(out=ksT2[sl_][:, :, 2048:2056], in_=sT[:, 3:6, 8 * bl:8 * bl + 8]), reads=["sT", "ksT%d" % sl_], writes=["ksT%d" % sl_])
                        hst = {}
                        for half in range(2):
                            s_ = nxt("pS")
                            hst[half] = s_
                            for h3 in range(3):
                                h = 3 * half + h3
                                ft, hh = h // 2, h % 2
                                rows = slice(64 * hh, 64 * hh + 64)
                                qs = qsz[:, h, 8 * bl:8 * bl + 8]
                                c0 = 136 * h3
                                for kt in range(16):
                                    S.op("pe", lambda e, kt=kt, ft=ft, rows=rows, qs=qs, c0=c0, h3=h3: e.matmul(PA(s_)[:, 24 * kt + 8 * h3:24 * kt + 8 * h3 + 8], ksT2[sl_][:, ft, 128 * kt:128 * kt + 128], qs,
                                                                                                      start=True, stop=True),
                                         reads=["ksT%d" % sl_, "qsz"], writes=["pB%d" % s_])
                                S.op("pe", lambda e, ft=ft, rows=rows, qs=qs, c0=c0, h3=h3: e.matmul(PA(s_)[0:8, 384 + 8 * h3:384 + 8 * h3 + 8], ksT2[sl_][:, ft, 2048:2056], qs, start=True, stop=True),
                                     reads=["ksT%d" % sl_, "qsz"], writes=["pB%d" % s_])
                        for half in range(2):
                            s_ = hst[half]
                            S.op("act", lambda e, half=half, s_=s_: e.activation(out=Ew2[half][:], in_=PA(s_)[:, 0:408], func=AF.Exp, scale=0.125), reads=["pB%d" % s_], writes=["Ew_%d" % half])
                            S.op("dve", lambda e, half=half: e.tensor_tensor(out=Ew2[half][:], in0=Ew2[half][:], in1=Cm3f, op=ALU.mult), reads=["Ew_%d" % half, "Cm3"], writes=["Ew_%d" % half])
                        ost = {}
                        for half in range(2):
                            oslot = nxt("pO")
                            ost[half] = oslot
                            for h3 in range(3):
                                h = 3 * half + h3
                                hh = h % 2
                                rows = slice(64 * hh, 64 * hh + 64)
                                c0 = 136 * h3
                                for kt in range(16):
                                    S.op("pe", lambda e, kt=kt, h=h, h3=h3, rows=rows, c0=c0, half=half: e.matmul(PA(oslot)[rows, 8 * h3:8 * h3 + 8], cvb2[sl_][:, kt, 64 * h:64 * h + 64],
                                                                                                               Ew2[half][:, 24 * kt + 8 * h3:24 * kt + 8 * h3 + 8],
                                                                                                               start=(kt == 0), stop=False, skip_group_check=True),
                                         reads=["cvb%d" % sl_, "Ew_%d" % half], writes=["pB%d" % oslot])
                                S.op("pe", lambda e, h=h, h3=h3, rows=rows, c0=c0, half=half: e.matmul(PA(oslot)[rows, 8 * h3:8 * h3 + 8], vnew[0:8, bl, 64 * h:64 * h + 64],
                                                                                                    Ew2[half][0:8, 384 + 8 * h3:384 + 8 * h3 + 8], start=False, stop=True, skip_group_check=True),
                                     reads=["vnew", "Ew_%d" % half], writes=["pB%d" % oslot])
                            for hf in range(2):
                                hr = slice(64 * hf, 64 * hf + 64)
                                for kt in range(16):
                                    S.op("pe", lambda e, half=half, kt=kt, hr=hr: e.matmul(PA(oslot)[hr, 32:56], onesb[:, 0:64], Ew2[half][:, 24 * kt:24 * kt + 24], start=(kt == 0), stop=False, skip_group_check=True),
                                         reads=["onesb", "Ew_%d" % half], writes=["pB%d" % oslot])
                                S.op("pe", lambda e, half=half, hr=hr: e.matmul(PA(oslot)[hr, 32:56], onesb[0:8, 0:64], Ew2[half][0:8, 384:408], start=False, stop=True, skip_group_check=True),
                                     reads=["onesb", "Ew_%d" % half], writes=["pB%d" % oslot])
                        if bl + 2 < 4:
                            load_bl(bl + 2)
                        for half in range(2):
                            oslot = ost[half]
                            for hf in range(2):
                                hr = slice(64 * hf, 64 * hf + 64)
                                S.op("dve", lambda e, half=half, oslot=oslot, hr=hr: e.reciprocal(out=zs2[half][hr].rearrange("p h q -> p (h q)"), in_=PA(oslot)[hr, 32:56]),
                                     reads=["pB%d" % oslot, "zs_%d" % half], writes=["zs_%d" % half])
                            for h3 in range(3):
                                h = 3 * half + h3
                                ft, hh = h // 2, h % 2
                                rows = slice(64 * hh, 64 * hh + 64)
                                S.op("dve", lambda e, half=half, oslot=oslot, h3=h3, ft=ft, rows=rows: e.tensor_tensor(out=On[rows, ft, 8 * bl:8 * bl + 8], in0=PA(oslot)[rows, 8 * h3:8 * h3 + 8],
                                                                                                               in1=zs2[half][rows, h3, :], op=ALU.mult),
                                     reads=["pB%d" % oslot, "zs_%d" % half, "On"], writes=["On"])
                    S.op("dve", lambda e: e.tensor_tensor(out=mixTs[:, 0:3, :], in0=On[:], in1=sT[:, 11:14, :], op=ALU.mult), reads=["On", "sT"], writes=["mixTs"])
                    S.barrier()
                chk(7)

                with ExitStack() as p5:
                    wo = sbt(p5, "wo", [128, 8, 1024], BF16)
                    gpost = sbt(p5, "gpost", [128, 1024], F32)
                    xr = [sbt(p5, "xr%d" % i, [128, 1024], F32) for i in range(2)]
                    yo = [sbt(p5, "yo%d" % i, [128, 1024], F32) for i in range(2)]
                    junk2 = sbt(p5, "junk2", [128, 512], BF16)
                    ss2 = sbt(p5, "ss2", [128, 64], F32)
                    st2 = sbt(p5, "st2", [128, 32], F32)
                    rs2 = sbt(p5, "rs2", [128, 32], F32)
                    S.dma("pool", wo[:], w_out.rearrange("(c p) n -> p c n", p=128), writes=["wo"])
                    S.dma("sp", gpost[:], g_post[0:1, :].to_broadcast([128, 1024]), writes=["gpost"])
                    S.op("pool", lambda e: e.memset(ss2[:], 0.0), writes=["ss2"])
                    roles.update({"pA": [0, 1, 2, 3, 4, 5, 6, 7]})
                    bulk_issue(len(bulk_jobs))

                    def out_tile(idx, lhs_fn, npart, x_src, y_dst):
                        sl_ = idx % 2
                        S.dma("sp", xr[sl_][0:npart, :], x_src, writes=["xr%d" % sl_])
                        pas = []
                        for half in range(2):
                            pa = nxt("pA")
                            pas.append(pa)
                            for c in range(8):
                                S.op("pe", lambda e, c=c, half=half, pa=pa: e.matmul(PA(pa)[0:npart, :], lhs_fn(c), wo[:, c, 512 * half:512 * half + 512], start=(c == 0), stop=(c == 7)),
                                     reads=["wo", "mixall"], writes=["pB%d" % pa])
                            S.op("act", lambda e, half=half, pa=pa: e.activation(out=junk2[0:npart, :], in_=PA(pa)[0:npart, :], func=AF.Square,
                                                                                accum_out=ss2[0:npart, 2 * idx + half:2 * idx + half + 1]),
                                 reads=["pB%d" % pa, "ss2"], writes=["junk2", "ss2_%d_%d" % (idx, half)])
                            hs_ = slice(512 * half, 512 * half + 512)
                            S.op("dve", lambda e, half=half, pa=pa, hs_=hs_: e.tensor_tensor(out=yo[sl_][0:npart, hs_], in0=PA(pa)[0:npart, :], in1=gpost[0:npart, hs_], op=ALU.mult),
                                 reads=["pB%d" % pa, "gpost", "yo%d" % sl_], writes=["yo%d" % sl_])
                        S.op("pool", lambda e: e.tensor_tensor(out=st2[0:npart, idx:idx + 1], in0=ss2[0:npart, 2 * idx:2 * idx + 1], in1=ss2[0:npart, 2 * idx + 1:2 * idx + 2], op=ALU.add),
                             reads=["ss2_%d_0" % idx, "ss2_%d_1" % idx], writes=["st2_%d" % idx])
                        S.op("pool", lambda e: e.tensor_scalar(out=st2[0:npart, idx:idx + 1], in0=st2[0:npart, idx:idx + 1], scalar1=1.0 / 1024, scalar2=EPS, op0=ALU.mult, op1=ALU.add),
                             reads=["st2_%d" % idx], writes=["st2_%d" % idx])
                        S.op("pool", lambda e: e.tensor_tensor(out=rs2[0:npart, idx:idx + 1], in0=st2[0:npart, idx:idx + 1], in1=mhalf[0:npart, 0:1], op=ALU.pow),
                             reads=["st2_%d" % idx, "mhalf"], writes=["rs2_%d" % idx])
                        for half in range(2):
                            hs = slice(512 * half, 512 * half + 512)
                            S.op("dve", lambda e, half=half, hs=hs: e.scalar_tensor_tensor(out=yo[sl_][0:npart, hs], in0=yo[sl_][0:npart, hs], scalar=rs2[0:npart, idx:idx + 1],
                                                                                          in1=xr[sl_][0:npart, hs], op0=ALU.mult, op1=ALU.add),
                                 reads=["rs2_%d" % idx, "xr%d" % sl_, "yo%d" % sl_], writes=["yo%d" % sl_])
                        S.dma("sp", y_dst, yo[sl_][0:npart, :], reads=["yo%d" % sl_])

                    for tt in range(16):
                        out_tile(tt, lambda c, tt=tt: mixT[:, c, 128 * tt:128 * tt + 128], 128, xe[2048 + 128 * tt:2048 + 128 * tt + 128, :], yp[128 * tt:128 * tt + 128, :])
                    out_tile(16, lambda c: mixTs[:, c, :], 32, xs[:, :], ys[:, :])
                    S.barrier()
        except _Stop:
            pass
        import os
        S.dead = False
        if os.environ.get("KDEBUG"):
            dbg = nc.dram_tensor("dbg", [128, 8, 2048], F32, kind="ExternalOutput").ap()
            S.dma("pool", dbg[:], mixT[:], reads=["mix%d" % i for i in range(8)])
            dbgs = nc.dram_tensor("dbgs", [128, 8, 32], F32, kind="ExternalOutput").ap()
            S.dma("pool", dbgs[:], mixTs[:], reads=["mixTs"])
        S.finish()
    return nc


def _layout_inputs(inp):
    f32 = np.float32
    xp = np.asarray(inp["x_prompt"], f32)
    inv = (500000.0 ** (-np.arange(0, 16, 2, dtype=np.float32) / 16.0)).astype(f32)
    rcst = np.zeros((128, 2), f32)
    for p in range(128):
        i = p % 64
        if i < 16:
            rcst[p, 0] = inv[i % 8]
            rcst[p, 1] = -1.0 if i < 8 else 1.0
    rcs = np.zeros((32, 49), f32)
    rcs[:, 0] = 16384 + (np.arange(32) % 8)
    rcs[:, 1:] = np.tile(inv, 6)[None, :]
    vecs = np.stack([np.asarray(inp[k], f32).reshape(3, 128).T for k in ("b_dw", "ln_conv_g", "ln_conv_b", "b_pw2")], axis=1)
    w_dwT = np.ascontiguousarray(np.asarray(inp["w_dw"], f32)[0].T.reshape(3, 128, 31).transpose(1, 0, 2))
    shared = {
        "rcst": rcst, "rcs": rcs, "vecs": np.ascontiguousarray(vecs), "w_dwT": w_dwT,
        "w_in": np.asarray(inp["w_in"], f32)[0], "w_out": np.asarray(inp["w_out"], f32)[0],
        "w_mkv": np.asarray(inp["w_mem_kv"], f32)[0], "w_pw2": np.asarray(inp["w_pw2"], f32)[0],
        "g_pre": np.asarray(inp["norm_pre"], f32).reshape(1, 1024), "g_post": np.asarray(inp["norm_post"], f32).reshape(1, 1024),
        "g_mem": np.asarray(inp["norm_mem"], f32).reshape(1, 1024),
    }
    maps = []
    for core in range(8):
        b, c = core // 4, core % 4
        t0 = 2048 * c
        xe = np.zeros((4096, 1024), f32)
        if c > 0:
            xe[:2048] = xp[b, t0 - 2048:t0]
        xe[2048:] = xp[b, t0:t0 + 2048]
        posv = np.maximum(t0 - 2048 + np.arange(4096), 0).astype(f32).reshape(1, 4096)
        m = dict(shared)
        m.update({
            "xe": xe, "pos": posv, "hflag": np.full((128, 1), 1.0 if c > 0 else 0.0, f32),
            "xs": np.ascontiguousarray(np.asarray(inp["x_sample"], f32)[4 * core:4 * core + 4].reshape(32, 1024)),
            "ck": np.ascontiguousarray(np.asarray(inp["cache_win_k"], f32)[0, 4 * core:4 * core + 4].reshape(4, 2048, 384)),
            "cv": np.ascontiguousarray(np.asarray(inp["cache_win_v"], f32)[0, 4 * core:4 * core + 4].reshape(4, 2048, 384)),
            "sc": np.ascontiguousarray(np.asarray(inp["state_conv"], f32)[0, 4 * core:4 * core + 4]),
            "cmk": np.ascontiguousarray(np.asarray(inp["cache_mem_k"], f32)[0, 4 * core:4 * core + 4].reshape(4, 256, 256)),
            "cmv": np.ascontiguousarray(np.asarray(inp["cache_mem_v"], f32)[0, 4 * core:4 * core + 4].reshape(4, 256, 256)),
            "mem": np.ascontiguousarray(np.asarray(inp["mem_prompt"], f32)[b]),
        })
        maps.append(m)
    return maps


def kernel(**inp):
    maps = _layout_inputs(inp)
    nc = build_nc()
    res = run_bass_kernel_spmd(nc, maps, core_ids=list(range(8)))
    R = res.results
    f32 = np.float32
    y_p = np.zeros((2, 8192, 1024), f32)
    for core in range(8):
        b, c = core // 4, core % 4
        y_p[b, 2048 * c:2048 * c + 2048] = R[core]["yp"]
    y_s = np.concatenate([R[core]["ys"].reshape(4, 8, 1024) for core in range(8)], axis=0)
    kpo = np.stack([R[4 * b + 3]["kp"].reshape(2048, 6, 64) for b in range(2)])[None]
    vpo = np.stack([R[4 * b + 3]["vp"].reshape(2048, 6, 64) for b in range(2)])[None]
    cvo = np.stack([R[4 * b + 3]["cvp"] for b in range(2)])[None]
    mko = np.stack([R[4 * b]["mkp"].reshape(256, 4, 64) for b in range(2)])[None]
    mvo = np.stack([R[4 * b]["mvp"].reshape(256, 4, 64) for b in range(2)])[None]
    kso = np.concatenate([R[core]["ksn"].reshape(4, 2048, 6, 64) for core in range(8)], axis=0)[None]
    vso = np.concatenate([R[core]["vsn"].reshape(4, 2048, 6, 64) for core in range(8)], axis=0)[None]
    cso = np.concatenate([R[core]["csn"] for core in range(8)], axis=0)[None]
    return (y_p, y_s, kpo, vpo, cvo, mko, mvo, kso, vso, cso)
```
